# Optimizing a Trainium2 kernel written in Bass

```python
import jax, jax.numpy as jnp
from jax import lax
import numpy as np

D_MODEL = 1024
BATCH = 8
SEQ = 2048
DEPTH = 2

N_MIXERS = 2
N_RET_LAYERS = (DEPTH + 1) // 2
N_DSA_LAYERS = DEPTH // 2
RET_HEADS = 4
RET_DK = D_MODEL // RET_HEADS
RET_DV = 2 * RET_DK
RET_CHUNK = 128
RET_IN = 2 * RET_HEADS * RET_DK + 2 * RET_HEADS * RET_DV
ATT_HEADS = 8
ATT_DH = D_MODEL // ATT_HEADS
ATT_KV_HEADS = 2
IDX_HEADS = 8
IDX_DH = 64
TOPK_MAX = 256
Q_BLOCK = 64
DSA_IN = (ATT_HEADS * ATT_DH + 2 * ATT_KV_HEADS * ATT_DH
          + IDX_HEADS * IDX_DH + IDX_DH + IDX_HEADS)
D_FF = 4 * D_MODEL
ROPE_THETA = 10000.0
EPS = 1e-6

kernel_name = "hybrid_retention_dsa_trunk"


def rmsnorm(x, g):
    x32 = x.astype(jnp.float32)
    y = x32 * lax.rsqrt(jnp.mean(x32 * x32, axis=-1, keepdims=True) + EPS)
    return (y * g.astype(jnp.float32)).astype(x.dtype)


def rope(x, pos):
    half = x.shape[-1] // 2
    inv = ROPE_THETA ** (-jnp.arange(half, dtype=jnp.float32) / half)
    ang = pos.astype(jnp.float32)[..., None] * inv
    cos = jnp.cos(ang)[:, :, None, :]
    sin = jnp.sin(ang)[:, :, None, :]
    x32 = x.astype(jnp.float32)
    x1, x2 = x32[..., :half], x32[..., half:]
    out = jnp.concatenate([x1 * cos - x2 * sin, x2 * cos + x1 * sin], axis=-1)
    return out.astype(x.dtype)


def retention(h, w_in, out_gain, w_out, pos):
    B, S, _ = h.shape
    H, DK, DV, C = RET_HEADS, RET_DK, RET_DV, RET_CHUNK
    proj = h @ w_in
    q, k, v, g = jnp.split(proj, [H * DK, 2 * H * DK, 2 * H * DK + H * DV], axis=-1)
    q = rope(q.reshape(B, S, H, DK), pos)
    k = rope(k.reshape(B, S, H, DK), pos) * (DK ** -0.5)
    v = v.reshape(B, S, H, DV)
    N = S // C

    def to_chunks(t):
        return t.reshape(B, N, C, H, -1).transpose(1, 0, 3, 2, 4).astype(jnp.float32)

    qc, kc, vc = to_chunks(q), to_chunks(k), to_chunks(v)
    log_gamma = jnp.log1p(-(2.0 ** (-5.0 - jnp.arange(H, dtype=jnp.float32))))
    i = jnp.arange(C, dtype=jnp.float32)
    diff = i[:, None] - i[None, :]
    decay_mask = jnp.where(diff >= 0,
                           jnp.exp(log_gamma[:, None, None] * jnp.maximum(diff, 0.0)), 0.0)
    q_decay = jnp.exp(log_gamma[:, None] * (i + 1.0))
    k_decay = jnp.exp(log_gamma[:, None] * (C - 1.0 - i))
    chunk_decay = jnp.exp(log_gamma * C)

    def step(state, qkv):
        qb, kb, vb = qkv
        scores = jnp.einsum('bhid,bhjd->bhij', qb, kb) * decay_mask
        o = (jnp.einsum('bhij,bhjv->bhiv', scores, vb)
             + jnp.einsum('bhid,bhdv->bhiv', qb * q_decay[..., None], state))
        state = (state * chunk_decay[:, None, None]
                 + jnp.einsum('bhjd,bhjv->bhdv', kb * k_decay[..., None], vb))
        return state, o

    state0 = jnp.zeros((B, H, DK, DV), jnp.float32)
    _, o = lax.scan(step, state0, (qc, kc, vc))
    o = o.transpose(1, 0, 3, 2, 4).reshape(B, S, H, DV)
    o = rmsnorm(o, out_gain).reshape(B, S, H * DV).astype(h.dtype)
    return (o * jax.nn.silu(g)) @ w_out


def sparse_attention(h, w_in, q_gain, k_gain, kidx_gain, w_out, pos):
    B, S, _ = h.shape
    H, G, DH, HI, DI, QB = ATT_HEADS, ATT_KV_HEADS, ATT_DH, IDX_HEADS, IDX_DH, Q_BLOCK
    splits = np.cumsum([H * DH, G * DH, G * DH, HI * DI, DI]).tolist()
    q, k, v, qi, ki, wi = jnp.split(h @ w_in, splits, axis=-1)
    q = rope(rmsnorm(q.reshape(B, S, H, DH), q_gain), pos)
    k = rope(rmsnorm(k.reshape(B, S, G, DH), k_gain), pos)
    v = v.reshape(B, S, G, DH)
    qi = rope(qi.reshape(B, S, HI, DI), pos)
    ki = rope(rmsnorm(ki, kidx_gain)[:, :, None, :], pos)[:, :, 0, :]
    wi = wi * (HI ** -0.5 * DI ** -0.5)
    topk = min(TOPK_MAX, S // 4)
    nb = S // QB
    ki32 = ki.astype(jnp.float32)
    s_pos = jnp.arange(S)
    neg = jnp.finfo(jnp.float32).min

    def blocks(t):
        return t.reshape((B, nb, QB) + t.shape[2:]).swapaxes(0, 1)

    def block(args):
        blk, qb, qib, wb = args
        t = blk * QB + jnp.arange(QB)
        rel = jax.nn.relu(jnp.einsum('bqhd,bsd->bqhs', qib.astype(jnp.float32), ki32))
        score = jnp.einsum('bqhs,bqh->bqs', rel, wb.astype(jnp.float32))
        score = jnp.where((s_pos[None, :] <= t[:, None])[None], score, neg)
        _, idx = lax.top_k(score, topk)
        valid = idx <= t[None, :, None]
        ks = jax.vmap(lambda kk, ii: kk[ii])(k, idx)
        vs = jax.vmap(lambda vv, ii: vv[ii])(v, idx)
        qg = qb.reshape(B, QB, G, H // G, DH)
        logits = jnp.einsum('bqgrd,bqkgd->bqgrk', qg, ks).astype(jnp.float32) * (DH ** -0.5)
        logits = jnp.where(valid[:, :, None, None, :], logits, neg)
        p = jax.nn.softmax(logits, axis=-1).astype(vs.dtype)
        return jnp.einsum('bqgrk,bqkgd->bqgrd', p, vs).reshape(B, QB, H * DH)

    out = lax.map(block, (jnp.arange(nb), blocks(q), blocks(qi), blocks(wi)))
    return out.swapaxes(0, 1).reshape(B, S, H * DH) @ w_out


def sq_relu_mlp(h, w_up, w_down):
    return jnp.square(jax.nn.relu(h @ w_up)) @ w_down


def setup_inputs(seed: int = 0) -> dict:
    key = jax.random.key(seed)
    ks = jax.random.split(key, 16)
    nrm = lambda k, shape, scale: jax.random.normal(k, shape, jnp.float32) * scale
    gain = lambda k, shape: 1.0 + 0.02 * jax.random.normal(k, shape, jnp.float32)
    x = jax.random.normal(ks[0], (BATCH, SEQ, D_MODEL), jnp.float32)
    positions = jnp.broadcast_to(jnp.arange(SEQ, dtype=jnp.int32), (BATCH, SEQ))
    return {
        "x": x,
        "positions": positions,
        "attn_norm": gain(ks[1], (DEPTH, D_MODEL)),
        "ret_w_in": nrm(ks[2], (N_RET_LAYERS, D_MODEL, RET_IN), D_MODEL ** -0.5),
        "ret_out_norm": gain(ks[3], (N_RET_LAYERS, RET_HEADS, RET_DV)),
        "ret_w_out": nrm(ks[4], (N_RET_LAYERS, RET_HEADS * RET_DV, D_MODEL), (RET_HEADS * RET_DV) ** -0.5),
        "dsa_w_in": nrm(ks[5], (N_DSA_LAYERS, D_MODEL, DSA_IN), D_MODEL ** -0.5),
        "dsa_q_norm": gain(ks[6], (N_DSA_LAYERS, ATT_DH)),
        "dsa_k_norm": gain(ks[7], (N_DSA_LAYERS, ATT_DH)),
        "dsa_kidx_norm": gain(ks[8], (N_DSA_LAYERS, IDX_DH)),
        "dsa_w_out": nrm(ks[9], (N_DSA_LAYERS, ATT_HEADS * ATT_DH, D_MODEL), (ATT_HEADS * ATT_DH) ** -0.5),
        "mlp_norm": gain(ks[10], (DEPTH, D_MODEL)),
        "mlp_w_up": nrm(ks[11], (DEPTH, D_MODEL, D_FF), D_MODEL ** -0.5),
        "mlp_w_down": nrm(ks[12], (DEPTH, D_FF, D_MODEL), D_FF ** -0.5),
    }


def reference(x, positions, attn_norm, ret_w_in, ret_out_norm, ret_w_out,
              dsa_w_in, dsa_q_norm, dsa_k_norm, dsa_kidx_norm, dsa_w_out,
              mlp_norm, mlp_w_up, mlp_w_down):
    for i in range(DEPTH):
        h = rmsnorm(x, attn_norm[i])
        j = i // N_MIXERS
        if i % N_MIXERS == 0:
            x = x + retention(h, ret_w_in[j], ret_out_norm[j], ret_w_out[j], positions)
        else:
            x = x + sparse_attention(h, dsa_w_in[j], dsa_q_norm[j], dsa_k_norm[j],
                                     dsa_kidx_norm[j], dsa_w_out[j], positions)
        h = rmsnorm(x, mlp_norm[i])
        x = x + sq_relu_mlp(h, mlp_w_up[i], mlp_w_down[i])
    return x
```

```python
import numpy as np
from contextlib import ExitStack

import concourse.bass as bass
import concourse.mybir as mybir
from concourse.bass_utils import run_bass_kernel_spmd

F32 = mybir.dt.float32
BF16 = mybir.dt.bfloat16
I32 = mybir.dt.int32
ALU = mybir.AluOpType
AF = mybir.ActivationFunctionType
AX = mybir.AxisListType

NCORES = 8
S = 2048
D = 1024
NT = S // 128
KT = D // 128
DFF = 4096
EPS = 1e-6
RH, RDK, RDV = 4, 256, 512
RET_IN = 6144
AH, ADH, AG = 8, 128, 2
IH, IDH = 8, 64
DSA_IN = 2120
TOPK = 256
NBIS = 16
DEBUG = False
LAST = {}


class Trk:
    __slots__ = ("name", "w", "r", "dsem")

    def __init__(self, name=""):
        self.name = name
        self.w = None
        self.r = {}
        self.dsem = None


class DSem:
    __slots__ = ("h", "count", "q")

    def __init__(self, h, q):
        self.h = h
        self.count = 0
        self.q = q


class Sched:
    COMPUTE = ("pe", "act", "dve", "pool")
    ALL = ("pe", "act", "dve", "pool", "sp")

    def __init__(self, nc, es):
        self.nc = nc
        self.es = es
        self.ops = {e: [] for e in self.ALL}
        self.count = {e: 0 for e in self.COMPUTE}
        self.sem = {e: es.enter_context(nc.semaphore("sem_" + e)) for e in self.COMPUTE}
        self.seen = {e: {} for e in self.ALL}
        self.ndsem = 0

    def new_dsem(self, q):
        self.ndsem += 1
        return DSem(self.es.enter_context(self.nc.semaphore("dsem%d" % self.ndsem)), q)

    def _semh(self, key):
        return self.sem[key] if isinstance(key, str) else key.h

    def _collect(self, eng, reads, writes, is_dma=False):
        deps = {}

        def add(ev, raw):
            if ev is None:
                return
            key, val = ev
            if isinstance(key, str) and key == eng and not is_dma:
                if eng == "pe":
                    return
            if deps.get(key, 0) < val:
                deps[key] = val

        for t in reads:
            add(t.w, True)
        for t in writes:
            add(t.w, False)
            for k, v in t.r.items():
                add((k, v), False)
        waits = []
        seen = self.seen[eng]
        for key, val in deps.items():
            if seen.get(key, 0) < val:
                seen[key] = val
                waits.append((key, val))
        return waits

    def op(self, eng, fn, reads=(), writes=()):
        waits = self._collect(eng, reads, writes)
        self.count[eng] += 1
        ev = (eng, self.count[eng])
        for t in reads:
            if t.r.get(eng, 0) < ev[1]:
                t.r[eng] = ev[1]
        for t in writes:
            t.w = ev
            t.r = {}
        self.ops[eng].append((waits, fn, ("c", eng, ev[1])))

    def dma(self, eng, fn, owner, reads=(), writes=()):
        waits = self._collect(eng, reads, writes, is_dma=True)
        if owner.dsem is None:
            owner.dsem = self.new_dsem(eng)
        ds = owner.dsem
        assert ds.q == eng, "DMA semaphore shared between issuing queues"

        ds.count += 16
        ev = (ds, ds.count)
        for t in reads:
            if t.r.get(ds, 0) < ev[1]:
                t.r[ds] = ev[1]
        for t in writes:
            t.w = ev
            t.r = {}
        self.ops[eng].append((waits, fn, ("d", ds.h, 16)))

    def wait_all(self, eng, trks):
        waits = self._collect(eng, (), trks, is_dma=True)
        if waits:
            self.ops[eng].append((waits, None, None))

    def emit(self):
        nc = self.nc
        needed = {e: set() for e in self.COMPUTE}
        for name in self.ALL:
            for waits, fn, inc in self.ops[name]:
                for key, val in waits:
                    if isinstance(key, str):
                        needed[key].add(val)
        rank = {e: {t: i + 1 for i, t in enumerate(sorted(needed[e]))} for e in self.COMPUTE}

        def replay(name):
            def run(eng):
                for waits, fn, inc in self.ops[name]:
                    for key, val in waits:
                        if isinstance(key, str):
                            eng.wait_ge(self.sem[key], rank[key][val])
                        else:
                            eng.wait_ge(key.h, val)
                    if fn is not None:
                        ins = fn(eng)
                        if inc[0] == "d":
                            ins.then_inc(inc[1], inc[2])
                        elif inc[2] in needed[inc[1]]:
                            ins.then_inc(self.sem[inc[1]], 1)
            return run

        with nc.Block() as block:
            block.sync(replay("sp"))
            block.tensor(replay("pe"))
            block.scalar(replay("act"))
            block.vector(replay("dve"))
            block.gpsimd(replay("pool"))


ARENA = 204 * 1024
U8 = mybir.dt.uint8
_DTSIZE = {F32: 4, BF16: 2, I32: 4}


class T:
    __slots__ = ("v", "t", "off", "size")

    def __init__(self, v, t, off, size):
        self.v, self.t, self.off, self.size = v, t, off, size

    def trks(self):
        return self.t if isinstance(self.t, list) else [self.t]


class Builder:
    def __init__(self, nc, es):
        self.nc = nc
        self.es = es
        self.sc = Sched(nc, es)
        self.arena = es.enter_context(nc.sbuf_tensor("arena", [128, ARENA], U8))
        self.free = [(0, ARENA)]
        self.ghosts = []
        self.free_dsems = {"sp": [], "pool": [], "act": []}
        self.npsum = 0
        self.peak = 0

    def sb(self, shape, dtype, name="", n=1):
        elems = 1
        for s in shape[1:]:
            elems *= s
        nbytes = (elems * _DTSIZE[dtype] + 63) // 64 * 64
        off = None
        for i, (o, sz) in enumerate(self.free):
            if sz >= nbytes:
                off = o
                if sz == nbytes:
                    self.free.pop(i)
                else:
                    self.free[i] = (o + nbytes, sz - nbytes)
                break
        if off is None:
            raise RuntimeError("SBUF arena full allocating %s %s (free=%s)" % (name, shape, self.free))
        self.peak = max(self.peak, off + nbytes)
        v = self.arena[:, off:off + elems * _DTSIZE[dtype]].bitcast(dtype)
        if len(shape) == 3:
            v = v.rearrange("p (a b) -> p a b", a=shape[1])
        elif len(shape) == 4:
            v = v.rearrange("p (a b c) -> p a b c", a=shape[1], b=shape[2])
        trks = [Trk(name) for _ in range(n)]
        keep = []
        for (go, gs, ev) in self.ghosts:
            if go < off + nbytes and off < go + gs:
                for tr in trks:
                    for k, val in ev.items():
                        if tr.r.get(k, 0) < val:
                            tr.r[k] = val
                if go >= off and go + gs <= off + nbytes:
                    continue
            keep.append((go, gs, ev))
        self.ghosts = keep
        return T(v, trks if n > 1 else trks[0], off, nbytes)

    def release(self, *tiles):
        for tile in tiles:
            ev = {}
            for tr in tile.trks():
                if tr.w is not None and ev.get(tr.w[0], 0) < tr.w[1]:
                    ev[tr.w[0]] = tr.w[1]
                for k, val in tr.r.items():
                    if ev.get(k, 0) < val:
                        ev[k] = val
                if tr.dsem is not None:
                    self.free_dsems[tr.dsem.q].append(tr.dsem)
                    tr.dsem = None
            self.ghosts.append((tile.off, tile.size, ev))
            self.free.append((tile.off, tile.size))
        self.free.sort()
        merged = []
        for o, sz in self.free:
            if merged and merged[-1][0] + merged[-1][1] == o:
                merged[-1] = (merged[-1][0], merged[-1][1] + sz)
            else:
                merged.append((o, sz))
        self.free = merged

    def ps(self, shape, dtype, name=None):
        self.npsum += 1
        return self.es.enter_context(self.nc.psum_tensor("%s_%d" % (name or "ps", self.npsum), list(shape), dtype))

    def op(self, eng, fn, reads=(), writes=()):
        self.sc.op(eng, fn, reads, writes)

    def dma(self, eng, out, in_, owner, reads=(), writes=()):
        if owner.dsem is None and self.free_dsems[eng]:
            ds = self.free_dsems[eng].pop()
            owner.dsem = ds
            if ds.count:
                for tr in list(writes) + [owner]:
                    if tr.r.get(ds, 0) < ds.count:
                        tr.r[ds] = ds.count
        self.sc.dma(eng, lambda e: e.dma_start(out=out, in_=in_), owner, reads, writes)


class Common:
    def __init__(self, B, consts):
        self.B = B
        self.ident = B.sb([128, 128], BF16, "ident")
        B.dma("pool", self.ident.v, consts["ident"], self.ident.t, writes=[self.ident.t])
        self.eps = B.sb([128, 1], F32, "eps")
        B.op("pool", lambda e: e.memset(self.eps.v, EPS), writes=[self.eps.t])
        self.mhalf = B.sb([128, 16], F32, "mhalf")
        B.op("pool", lambda e: e.memset(self.mhalf.v, -0.5), writes=[self.mhalf.t])
        self.pf = [B.ps([128, 512], F32, "pf") for _ in range(6)]
        self.pf_t = [Trk("pf%d" % i) for i in range(6)]
        self.pb = [B.ps([128, 1024], BF16, "pb") for _ in range(2)]
        self.pb_t = [Trk("pb%d" % i) for i in range(2)]
        self.pf_i = 0
        self.pb_i = 0

    def next_pf(self):
        i = self.pf_i
        self.pf_i = (i + 1) % 6
        return self.pf[i], self.pf_t[i]

    def next_pb(self):
        i = self.pb_i
        self.pb_i = (i + 1) % 2
        return self.pb[i], self.pb_t[i]


class NormScratch:
    def __init__(self, B):
        self.junk = B.sb([128, D], BF16, "junk")
        self.st = B.sb([128, 4], F32, "nstat", n=3)
        self.hb = B.sb([128, D], BF16, "hb")

    def tiles(self):
        return [self.junk, self.st, self.hb]


def norm_s1(B, C, x_ap, x_trk, gain, scr):
    st = scr.st
    B.op("act", lambda e: e.activation(out=scr.junk.v, in_=x_ap, func=AF.Square,
                                       scale=1.0 / 32.0, accum_out=st.v[:, 0:1]),
         reads=[x_trk], writes=[scr.junk.t, st.t[0]])
    B.op("pool", lambda e: e.tensor_scalar(out=st.v[:, 1:1+1], in0=st.v[:, 0:0+1], scalar1=EPS,
                                           scalar2=None, op0=ALU.add), reads=[st.t[0]], writes=[st.t[1]])
    B.op("pool", lambda e: e.tensor_tensor(out=st.v[:, 2:2+1], in0=st.v[:, 1:1+1],
                                           in1=C.mhalf.v[:, 0:1], op=ALU.pow), reads=[st.t[1], C.mhalf.t], writes=[st.t[2]])
    B.op("dve", lambda e: e.scalar_tensor_tensor(out=scr.hb.v, in0=x_ap, scalar=st.v[:, 2:3],
                                                 in1=gain.v, op0=ALU.mult, op1=ALU.mult),
         reads=[x_trk, st.t[2], gain.t], writes=[scr.hb.t])


def norm_s2(B, C, t, hT, scr):
    pb, pb_t = C.next_pb()
    for k in range(KT):
        B.op("pe", lambda e, k=k: e.transpose(out=pb[:, k * 128:(k + 1) * 128],
                                              in_=scr.hb.v[:, k * 128:(k + 1) * 128], identity=C.ident.v),
             reads=[scr.hb.t, C.ident.t], writes=[pb_t])
    B.op("act", lambda e: e.copy(out=hT.v[:, :, t * 128:(t + 1) * 128],
                                 in_=pb[:].rearrange("p (k n) -> p k n", k=KT)),
         reads=[pb_t], writes=[hT.t[t]])


NLAG = 3


class DX:
    def __init__(self, ap):
        self.ap = ap
        self.t = [Trk("dx") for _ in range(NT)]

    def tile(self, t):
        return self.ap[t * 128:(t + 1) * 128, :]


def bc3(ap2d, n):
    return ap2d.unsqueeze(1).to_broadcast([ap2d.shape[0], n, ap2d.shape[1]])


def sin_table(B, dst_ap, dst_trk, ang_fn, shape):
    import math
    TWO_PI = 2 * math.pi
    a0 = B.sb(shape, F32, "a0")
    ki = B.sb(shape, I32, "ki")
    kf = B.sb(shape, F32, "kf")
    ang_fn(a0)
    B.op("dve", lambda e: e.tensor_scalar(out=ki.v, in0=a0.v, scalar1=1.0 / TWO_PI, scalar2=None, op0=ALU.mult),
         reads=[a0.t], writes=[ki.t])
    B.op("dve", lambda e: e.tensor_copy(out=kf.v, in_=ki.v), reads=[ki.t], writes=[kf.t])
    B.op("dve", lambda e: e.scalar_tensor_tensor(out=a0.v, in0=kf.v, scalar=-TWO_PI, in1=a0.v,
                                                 op0=ALU.mult, op1=ALU.add), reads=[kf.t, a0.t], writes=[a0.t])
    B.op("dve", lambda e: e.tensor_scalar(out=kf.v, in0=a0.v, scalar1=math.pi, scalar2=-TWO_PI,
                                          op0=ALU.is_gt, op1=ALU.mult), reads=[a0.t], writes=[kf.t])
    B.op("dve", lambda e: e.tensor_tensor(out=a0.v, in0=a0.v, in1=kf.v, op=ALU.add), reads=[a0.t, kf.t], writes=[a0.t])
    B.op("dve", lambda e: e.tensor_scalar(out=a0.v, in0=a0.v, scalar1=-math.pi, scalar2=math.pi,
                                          op0=ALU.max, op1=ALU.min), reads=[a0.t], writes=[a0.t])
    B.op("act", lambda e: e.activation(out=dst_ap, in_=a0.v, func=AF.Sin), reads=[a0.t], writes=[dst_trk])
    B.release(a0, ki, kf)


def load_norm_transpose_all(B, C, xin, gain_dram, hT, keep=None):
    gain = B.sb([128, D], F32, "gain")
    B.dma("sp", gain.v, gain_dram.partition_broadcast(128), gain.t, writes=[gain.t])
    scr = [NormScratch(B) for _ in range(NLAG + 1)]
    xb = None
    if keep is None:
        xb = [B.sb([128, D], F32, "xb") for _ in range(3)]
    for t in range(NT + NLAG):
        if t < NT:
            if keep is not None:
                xap, xt = keep.v[:, t, :], keep.t[t]
            else:
                xap, xt = xb[t % 3].v, xb[t % 3].t
            B.dma("sp", xap, xin.tile(t), xt, reads=[xin.t[t]], writes=[xt])
            norm_s1(B, C, xap, xt, gain, scr[t % (NLAG + 1)])
        if t - NLAG >= 0:
            norm_s2(B, C, t - NLAG, hT, scr[(t - NLAG) % (NLAG + 1)])
    for s in scr:
        B.release(*s.tiles())
    B.release(gain)
    if xb:
        B.release(*xb)


def mlp_prefetch_w0(B, w_up, w_down):
    FG = 512
    wup0 = B.sb([128, KT, FG], BF16, "wup")
    wdn0 = B.sb([128, FG // 128, D], BF16, "wdn")
    B.dma("pool", wup0.v, w_up[:, 0:FG].rearrange("(k p) f -> p k f", p=128), wup0.t, writes=[wup0.t])
    B.dma("pool", wdn0.v, w_down[0:FG, :].rearrange("(j p) n -> p j n", p=128), wdn0.t, writes=[wdn0.t])
    return wup0, wdn0


def mlp_phase(B, C, xin, xout, mlp_norm, w_up, w_down, pre=None, next_gain=None, tail_hook=None, w_pre=None):
    if pre is not None:
        xacc, hT = pre
        if xacc is None:
            xacc = B.sb([128, NT, D], F32, "xacc", n=NT)
            for t in range(NT):
                B.dma("sp", xacc.v[:, t, :], xin.tile(t), xacc.t[t], reads=[xin.t[t]], writes=[xacc.t[t]])
    else:
        xacc = B.sb([128, NT, D], F32, "xacc", n=NT)
        hT = B.sb([128, KT, S], BF16, "hT", n=NT)
        load_norm_transpose_all(B, C, xin, mlp_norm, hT, keep=xacc)
    nn = None
    hook_res = None
    released = []

    FG = 512
    NJ = FG // 128
    NG = DFF // FG
    if w_pre is not None:
        wup = [w_pre[0], B.sb([128, KT, FG], BF16, "wup")]
        wdn = [w_pre[1], B.sb([128, NJ, D], BF16, "wdn")]
    else:
        wup = [B.sb([128, KT, FG], BF16, "wup") for _ in range(2)]
        wdn = [B.sb([128, NJ, D], BF16, "wdn") for _ in range(2)]
    uT = B.sb([128, NJ, S], BF16, "uT", n=NJ * 4)
    rl = [B.sb([128, 512], F32, "rl") for _ in range(2)]
    rli = 0
    for g in range(NG):
        b = g % 2
        if not (g == 0 and w_pre is not None):
            B.dma("pool", wup[b].v, w_up[:, g * FG:(g + 1) * FG].rearrange("(k p) f -> p k f", p=128),
                  wup[b].t, writes=[wup[b].t])
            B.dma("pool", wdn[b].v, w_down[g * FG:(g + 1) * FG, :].rearrange("(j p) n -> p j n", p=128),
                  wdn[b].t, writes=[wdn[b].t])
        for j in range(NJ):
            for tb in range(4):
                pf, pf_t = C.next_pf()
                for k in range(KT):
                    B.op("pe", lambda e, k=k, j=j, tb=tb, pf=pf, b=b: e.matmul(
                        out=pf[:], lhsT=wup[b].v[:, k, j * 128:(j + 1) * 128],
                        rhs=hT.v[:, k, tb * 512:(tb + 1) * 512], start=(k == 0), stop=(k == KT - 1)),
                        reads=[wup[b].t] + hT.t[tb * 4:tb * 4 + 4], writes=[pf_t])
                r = rl[rli]
                rli ^= 1
                B.op("act", lambda e, pf=pf, r=r: e.activation(out=r.v, in_=pf[:], func=AF.Relu),
                     reads=[pf_t], writes=[r.t])
                B.op("pool", lambda e, r=r, j=j, tb=tb: e.tensor_tensor(
                    out=uT.v[:, j, tb * 512:(tb + 1) * 512], in0=r.v, in1=r.v, op=ALU.mult),
                    reads=[r.t], writes=[uT.t[j * 4 + tb]])
        if g == NG - 1 and next_gain is not None:
            released = [hT, *wup, wdn[1 - b], *rl]
            B.release(*released)
            nn = NextNorm(B, C, next_gain)
            if tail_hook is not None:
                hook_res = tail_hook()
        for t in range(NT):
            for half in range(2):
                pf, pf_t = C.next_pf()
                for j in range(NJ):
                    B.op("pe", lambda e, j=j, t=t, half=half, pf=pf, b=b: e.matmul(
                        out=pf[:], lhsT=uT.v[:, j, t * 128:(t + 1) * 128],
                        rhs=wdn[b].v[:, j, half * 512:(half + 1) * 512],
                        start=(j == 0), stop=(j == NJ - 1)),
                        reads=[wdn[b].t, uT.t[j * 4 + t // 4]], writes=[pf_t])
                B.op("dve", lambda e, t=t, half=half, pf=pf: e.tensor_tensor(
                    out=xacc.v[:, t, half * 512:(half + 1) * 512], in0=xacc.v[:, t, half * 512:(half + 1) * 512],
                    in1=pf[:], op=ALU.add),
                    reads=[pf_t, xacc.t[t]], writes=[xacc.t[t]])
            if g == NG - 1:
                B.dma("sp", xout.tile(t), xacc.v[:, t, :], xacc.t[t], reads=[xacc.t[t]], writes=[xout.t[t]])
                if nn is not None:
                    nn.feed(t, xacc.v[:, t, :], xacc.t[t])
    B.release(*[x for x in (xacc, hT, uT, *wup, *wdn, *rl) if x not in released])
    return (nn.finish() if nn is not None else None), hook_res


class NextNorm:
    def __init__(self, B, C, gain_dram):
        self.B, self.C = B, C
        self.hT = B.sb([128, KT, S], BF16, "hTn", n=NT)
        self.gain = B.sb([128, D], F32, "gainn")
        B.dma("sp", self.gain.v, gain_dram.partition_broadcast(128), self.gain.t, writes=[self.gain.t])
        self.scr = [NormScratch(B) for _ in range(NLAG + 1)]
        self.pending = []
        self.nfed = 0

    def feed(self, t, x_ap, x_trk):
        scr = self.scr[self.nfed % (NLAG + 1)]
        self.nfed += 1
        norm_s1(self.B, self.C, x_ap, x_trk, self.gain, scr)
        self.pending.append((t, scr))
        if len(self.pending) > NLAG:
            t2, s2 = self.pending.pop(0)
            norm_s2(self.B, self.C, t2, self.hT, s2)

    def finish(self):
        for t2, s2 in self.pending:
            norm_s2(self.B, self.C, t2, self.hT, s2)
        self.pending = []
        for s in self.scr:
            self.B.release(*s.tiles())
        self.B.release(self.gain)
        return self.hT


class DramSink:
    def __init__(self, B, xout, nn=None, nbuf=2):
        self.B, self.xout, self.nn = B, xout, nn
        self.xb = [B.sb([128, D], F32, "xb") for _ in range(nbuf)]

    def xtile(self, t):
        x = self.xb[t % len(self.xb)]
        return x.v, x.t

    def done(self, t):
        ap, trk = self.xtile(t)
        self.B.dma("sp", self.xout.tile(t), ap, trk, reads=[trk], writes=[self.xout.t[t]])
        if self.nn is not None:
            self.nn.feed(t, ap, trk)

    def finish(self):
        self.B.release(*self.xb)


class AccSink:
    def __init__(self, B, xacc, nn):
        self.B, self.xacc, self.nn = B, xacc, nn

    def xtile(self, t):
        return self.xacc.v[:, t, :], self.xacc.t[t]

    def done(self, t):
        ap, trk = self.xtile(t)
        self.nn.feed(t, ap, trk)

    def finish(self):
        pass


def out_proj_load_w(B, w_out, nk):
    parts = []
    for p0 in range(0, nk, 4):
        wp = B.sb([128, 4, D], BF16, "wo")
        B.dma("pool", wp.v, w_out[p0 * 128:(p0 + 4) * 128, :].rearrange("(j p) n -> p j n", p=128), wp.t,
              writes=[wp.t])
        parts.append(wp)
    return parts


def out_proj_phase(B, C, og, og_t, nk, w_out, xin, sink, wo_parts=None):
    if wo_parts is None:
        wo_parts = out_proj_load_w(B, w_out, nk)
    ogb = [B.sb([128, nk * 128], BF16, "ogb") for _ in range(2)]
    ogT = [B.sb([128, nk, 128], BF16, "ogT") for _ in range(2)]

    def T_stage(t):
        b = t % 2
        xap, xt = sink.xtile(t)
        B.dma("sp", ogb[b].v, og[t * 128:(t + 1) * 128, :], ogb[b].t, reads=og_t[t], writes=[ogb[b].t])
        B.dma("sp", xap, xin.tile(t), xt, reads=[xin.t[t]], writes=[xt])
        for j0 in range(0, nk, 8):
            pb, pb_t = C.next_pb()
            for j in range(j0, min(nk, j0 + 8)):
                B.op("pe", lambda e, j=j, j0=j0, pb=pb, b=b: e.transpose(
                    out=pb[:, (j - j0) * 128:(j - j0 + 1) * 128], in_=ogb[b].v[:, j * 128:(j + 1) * 128],
                    identity=C.ident.v), reads=[ogb[b].t, C.ident.t], writes=[pb_t])
            nj = min(nk, j0 + 8) - j0
            B.op("act", lambda e, j0=j0, nj=nj, pb=pb, b=b: e.copy(
                out=ogT[b].v[:, j0:j0 + nj, :], in_=pb[:, 0:nj * 128].rearrange("p (k n) -> p k n", k=nj)),
                reads=[pb_t], writes=[ogT[b].t])

    def M_stage(t):
        b = t % 2
        xap, xt = sink.xtile(t)
        for half in range(2):
            pf, pf_t = C.next_pf()
            for j in range(nk):
                B.op("pe", lambda e, j=j, half=half, pf=pf, b=b: e.matmul(
                    out=pf[:], lhsT=ogT[b].v[:, j, :], rhs=wo_parts[j // 4].v[:, j % 4, half * 512:(half + 1) * 512],
                    start=(j == 0), stop=(j == nk - 1)), reads=[ogT[b].t, wo_parts[j // 4].t], writes=[pf_t])
            B.op("dve", lambda e, half=half, pf=pf, xap=xap: e.tensor_tensor(
                out=xap[:, half * 512:(half + 1) * 512], in0=xap[:, half * 512:(half + 1) * 512],
                in1=pf[:], op=ALU.add), reads=[pf_t, xt], writes=[xt])
        sink.done(t)

    T_stage(0)
    for t in range(NT):
        if t + 1 < NT:
            T_stage(t + 1)
        M_stage(t)
    B.release(*wo_parts, *ogb, *ogT)
    sink.finish()


def ret_phase(B, C, xin, xout, pos, attn_norm, w_in, out_norm, w_out, K, next_gain=None, next_w_hook=None):
    import math
    nc = B.nc
    og = nc.dram_tensor("og_ret", [S, RH * RDV], BF16, kind="ExternalOutput" if DEBUG else "Internal").ap()
    og_t = [[Trk("og") for _ in range(RH)] for _ in range(NT)]
    cos = B.sb([128, S], F32, "cos")
    sin = B.sb([128, S], F32, "sin")
    posi = B.sb([128, S], I32, "posi")
    posf = B.sb([128, S], F32, "posf")
    inv = B.sb([128, 1], F32, "inv")
    B.dma("sp", posi.v, pos.partition_broadcast(128), posi.t, writes=[posi.t])
    B.dma("sp", inv.v, K["inv128"], inv.t, writes=[inv.t])
    B.op("dve", lambda e: e.tensor_copy(out=posf.v, in_=posi.v), reads=[posi.t], writes=[posf.t])

    def ang(shift):
        def f(a0):
            B.op("dve", lambda e: e.tensor_scalar(out=a0.v, in0=posf.v, scalar1=inv.v[:, 0:1], scalar2=shift,
                                                  op0=ALU.mult, op1=ALU.add), reads=[posf.t, inv.t], writes=[a0.t])
        return f
    sin_table(B, sin.v, sin.t, ang(0.0), [128, S])
    sin_table(B, cos.v, cos.t, ang(math.pi / 2), [128, S])
    B.release(posi, posf, inv)
    qd = B.sb([128, RH, 128], F32, "qd")
    kd = B.sb([128, RH, 128], F32, "kd")
    maskT = B.sb([128, 128], BF16, "maskT")
    gain_o = B.sb([128, RH, RDV], F32, "gain_o")
    B.dma("sp", qd.v, K["qdec"].partition_broadcast(128), qd.t, writes=[qd.t])
    B.dma("sp", kd.v, K["kdec"].partition_broadcast(128), kd.t, writes=[kd.t])
    B.dma("sp", maskT.v, K["maskT"], maskT.t, writes=[maskT.t])
    B.dma("sp", gain_o.v, out_norm.rearrange("h d -> (h d)").partition_broadcast(128), gain_o.t, writes=[gain_o.t])
    hT = B.sb([128, KT, S], BF16, "hT", n=NT)
    load_norm_transpose_all(B, C, xin, attn_norm, hT)
    wq = [B.sb([128, KT, RDK], BF16, "wq") for _ in range(2)]
    wk = [B.sb([128, KT, RDK], BF16, "wk") for _ in range(2)]
    wv = [B.sb([128, KT, RDV], BF16, "wv") for _ in range(2)]
    wg = [B.sb([128, KT, RDV], BF16, "wg") for _ in range(2)]
    qT2 = [B.sb([128, 2, S], BF16, "qT", n=4) for _ in range(2)]
    kT2 = [B.sb([128, 2, S], BF16, "kT", n=4) for _ in range(2)]
    ktm2 = [B.sb([128, NT, RDK], BF16, "ktm", n=4) for _ in range(2)]
    Tst = B.sb([128, 2, RDV], F32, "Tst", n=2)
    Sbf2 = [B.sb([128, 2, RDV], BF16, "Sbf", n=2) for _ in range(2)]
    ab = [[B.sb([128, 512], F32, "ab") for _ in range(2)] for _ in range(2)]
    tm = [B.sb([128, 512], F32, "tm") for _ in range(4)]
    vb = [B.sb([128, RDV], BF16, "vb") for _ in range(3)]
    sg = [B.sb([128, RDV], F32, "sg") for _ in range(3)]
    sgg = [B.sb([128, RDV], F32, "sgg") for _ in range(3)]
    sTm = [B.sb([128, 128], BF16, "sTm") for _ in range(3)]
    ogb = [B.sb([128, RDV], BF16, "ogb") for _ in range(2)]
    ost = [B.sb([128, 4], F32, "ost", n=3) for _ in range(2)]
    ojunk = B.sb([128, RDV], BF16, "ojunk")

    def load_w(h):
        b = h % 2
        for (wt, c0, wd) in ((wq[b], h * RDK, RDK), (wk[b], 1024 + h * RDK, RDK),
                             (wv[b], 2048 + h * RDV, RDV), (wg[b], 4096 + h * RDV, RDV)):
            B.dma("pool", wt.v, w_in[:, c0:c0 + wd].rearrange("(k p) f -> p k f", p=128), wt.t, writes=[wt.t])

    rs_ = {"ab": 0, "q": 0, "c": 0}

    def nbQ():
        i = 4 + rs_["q"]
        rs_["q"] ^= 1
        return C.pf[i], C.pf_t[i]

    def nbC():
        i = rs_["c"]
        rs_["c"] = (i + 1) % 4
        return C.pf[i], C.pf_t[i]

    def stage_QK(h):
        b = h % 2
        qT, kT, ktm = qT2[b], kT2[b], ktm2[b]
        for (wt, dec, dstT) in ((wq[b], qd, qT), (wk[b], kd, kT)):
            for tb in range(4):
                sl = slice(tb * 512, (tb + 1) * 512)
                A, Bm = ab[rs_['ab']]
                rs_['ab'] ^= 1
                for dti, dst in ((0, A), (1, Bm)):
                    pf, pf_t = nbQ()
                    for k in range(KT):
                        B.op("pe", lambda e, k=k, dti=dti, pf=pf, wt=wt, sl=sl: e.matmul(
                            out=pf[:], lhsT=wt.v[:, k, dti * 128:(dti + 1) * 128], rhs=hT.v[:, k, sl],
                            start=(k == 0), stop=(k == KT - 1)),
                            reads=[wt.t] + hT.t[tb * 4:tb * 4 + 4], writes=[pf_t])
                    B.op("dve", lambda e, pf=pf, dst=dst, dec=dec, h=h: e.tensor_tensor(
                        out=dst.v.rearrange("p (a b) -> p a b", a=4), in0=pf[:].rearrange("p (a b) -> p a b", a=4),
                        in1=bc3(dec.v[:, h, :], 4), op=ALU.mult), reads=[pf_t, dec.t], writes=[dst.t])
                B.op("pool", lambda e, A=A, sl=sl: e.tensor_tensor(out=tm[0].v, in0=A.v, in1=cos.v[:, sl], op=ALU.mult),
                     reads=[A.t, cos.t], writes=[tm[0].t])
                B.op("pool", lambda e, Bm=Bm, sl=sl: e.tensor_tensor(out=tm[1].v, in0=Bm.v, in1=sin.v[:, sl], op=ALU.mult),
                     reads=[Bm.t, sin.t], writes=[tm[1].t])
                B.op("dve", lambda e, dstT=dstT, sl=sl: e.tensor_tensor(out=dstT.v[:, 0, sl], in0=tm[0].v, in1=tm[1].v,
                                                                        op=ALU.subtract),
                     reads=[tm[0].t, tm[1].t], writes=[dstT.t[tb]])
                B.op("pool", lambda e, Bm=Bm, sl=sl: e.tensor_tensor(out=tm[2].v, in0=Bm.v, in1=cos.v[:, sl], op=ALU.mult),
                     reads=[Bm.t, cos.t], writes=[tm[2].t])
                B.op("pool", lambda e, A=A, sl=sl: e.tensor_tensor(out=tm[3].v, in0=A.v, in1=sin.v[:, sl], op=ALU.mult),
                     reads=[A.t, sin.t], writes=[tm[3].t])
                B.op("dve", lambda e, dstT=dstT, sl=sl: e.tensor_tensor(out=dstT.v[:, 1, sl], in0=tm[2].v, in1=tm[3].v,
                                                                        op=ALU.add),
                     reads=[tm[2].t, tm[3].t], writes=[dstT.t[tb]])
        for c4 in range(4):
            pb, pb_t = C.next_pb()
            for ci in range(4):
                c = c4 * 4 + ci
                for dti in range(2):
                    B.op("pe", lambda e, c=c, ci=ci, dti=dti, pb=pb: e.transpose(
                        out=pb[:, ci * 256 + dti * 128: ci * 256 + (dti + 1) * 128],
                        in_=kT.v[:, dti, c * 128:(c + 1) * 128], identity=C.ident.v),
                        reads=[kT.t[c4], C.ident.t], writes=[pb_t])
            B.op("act", lambda e, c4=c4, pb=pb: e.copy(
                out=ktm.v[:, c4 * 4:(c4 + 1) * 4, :], in_=pb[:].rearrange("p (c d) -> p c d", c=4)),
                reads=[pb_t], writes=[ktm.t[c4]])

    def stage_CH(h):
        b = h % 2
        qT, kT, ktm = qT2[b], kT2[b], ktm2[b]
        gam = 1.0 - 2.0 ** (-5.0 - h)
        cd = gam ** 128
        for dti in range(2):
            B.op("pool", lambda e, dti=dti: e.memset(Tst.v[:, dti, :], 0.0), writes=[Tst.t[dti]])

        def pre(c):
            cb = c % 3
            csl = slice(c * 128, (c + 1) * 128)
            pf, pf_t = nbC()
            for k in range(KT):
                B.op("pe", lambda e, k=k, pf=pf: e.matmul(
                    out=pf[:], lhsT=hT.v[:, k, csl], rhs=wv[b].v[:, k, :], start=(k == 0), stop=(k == KT - 1)),
                    reads=[wv[b].t, hT.t[c]], writes=[pf_t])
            B.op("act", lambda e, pf=pf: e.copy(out=vb[cb].v, in_=pf[:]), reads=[pf_t], writes=[vb[cb].t])
            pf, pf_t = nbC()
            for k in range(KT):
                B.op("pe", lambda e, k=k, pf=pf: e.matmul(
                    out=pf[:], lhsT=hT.v[:, k, csl], rhs=wg[b].v[:, k, :], start=(k == 0), stop=(k == KT - 1)),
                    reads=[wg[b].t, hT.t[c]], writes=[pf_t])
            B.op("act", lambda e, pf=pf: e.activation(out=sg[cb].v, in_=pf[:], func=AF.Silu),
                 reads=[pf_t], writes=[sg[cb].t])
            B.op("pool", lambda e: e.tensor_tensor(out=sgg[cb].v, in0=sg[cb].v, in1=gain_o.v[:, h, :], op=ALU.mult),
                 reads=[sg[cb].t, gain_o.t], writes=[sgg[cb].t])
            pf, pf_t = nbC()
            for dti in range(2):
                B.op("pe", lambda e, dti=dti, pf=pf: e.matmul(
                    out=pf[:, 0:128], lhsT=kT.v[:, dti, csl], rhs=qT.v[:, dti, csl], start=(dti == 0), stop=(dti == 1)),
                    reads=[kT.t[c // 4], qT.t[c // 4]], writes=[pf_t])
            B.op("dve", lambda e, pf=pf: e.tensor_tensor(out=sTm[cb].v, in0=pf[:, 0:128], in1=maskT.v, op=ALU.mult),
                 reads=[pf_t, maskT.t], writes=[sTm[cb].t])
            sb_ = Sbf2[c % 2]
            if c > 0:
                for dti in range(2):
                    B.op("act", lambda e, dti=dti: e.activation(out=sb_.v[:, dti, :], in_=Tst.v[:, dti, :],
                                                                func=AF.Copy, scale=cd),
                         reads=[Tst.t[dti]], writes=[sb_.t[dti]])
            if c < NT - 1:
                for dti in range(2):
                    pk, pk_t = nbC()
                    B.op("pe", lambda e, dti=dti, pk=pk: e.matmul(
                        out=pk[:], lhsT=ktm.v[:, c, dti * 128:(dti + 1) * 128], rhs=vb[cb].v, start=True, stop=True),
                        reads=[ktm.t[c // 4], vb[cb].t], writes=[pk_t])
                    B.op("dve", lambda e, dti=dti, pk=pk: e.scalar_tensor_tensor(
                        out=Tst.v[:, dti, :], in0=Tst.v[:, dti, :], scalar=cd, in1=pk[:], op0=ALU.mult, op1=ALU.add),
                        reads=[Tst.t[dti], pk_t], writes=[Tst.t[dti]])

        def post(c):
            cb = c % 3
            csl = slice(c * 128, (c + 1) * 128)
            sb_ = Sbf2[c % 2]
            po, po_t = nbC()
            B.op("pe", lambda e: e.matmul(out=po[:], lhsT=sTm[cb].v, rhs=vb[cb].v, start=True, stop=(c == 0)),
                 reads=[sTm[cb].t, vb[cb].t], writes=[po_t])
            if c > 0:
                for dti in range(2):
                    B.op("pe", lambda e, dti=dti: e.matmul(
                        out=po[:], lhsT=qT.v[:, dti, csl], rhs=sb_.v[:, dti, :], start=False, stop=(dti == 1)),
                        reads=[qT.t[c // 4], sb_.t[dti]], writes=[po_t])
            st = ost[c % 2]
            ob = ogb[c % 2]
            B.op("act", lambda e: e.activation(out=ojunk.v, in_=po[:], func=AF.Square,
                                               scale=1.0 / math.sqrt(RDV), accum_out=st.v[:, 0:1]),
                 reads=[po_t], writes=[ojunk.t, st.t[0]])
            B.op("pool", lambda e: e.tensor_scalar(out=st.v[:, 1:2], in0=st.v[:, 0:1], scalar1=EPS,
                                                   scalar2=None, op0=ALU.add), reads=[st.t[0]], writes=[st.t[1]])
            B.op("pool", lambda e: e.tensor_tensor(out=st.v[:, 2:3], in0=st.v[:, 1:2],
                                                   in1=C.mhalf.v[:, 0:1], op=ALU.pow),
                 reads=[st.t[1], C.mhalf.t], writes=[st.t[2]])
            B.op("dve", lambda e: e.scalar_tensor_tensor(
                out=ob.v, in0=po[:], scalar=st.v[:, 2:3], in1=sgg[cb].v, op0=ALU.mult, op1=ALU.mult),
                reads=[po_t, st.t[2], sgg[cb].t], writes=[ob.t])
            B.dma("sp", og[c * 128:(c + 1) * 128, h * RDV:(h + 1) * RDV], ob.v, ob.t,
                  reads=[ob.t], writes=[og_t[c][h]])

        pre(0)
        for c in range(NT):
            if c + 1 < NT:
                pre(c + 1)
            post(c)

    def record(fn, h):
        rec = []
        orig_op, orig_dma = B.op, B.dma
        B.op = lambda *a, **k: rec.append(("op", a, k))
        B.dma = lambda *a, **k: rec.append(("dma", a, k))
        try:
            fn(h)
        finally:
            B.op, B.dma = orig_op, orig_dma
        return [(lambda kind=kind, a=a, k=k: (orig_op if kind == "op" else orig_dma)(*a, **k)) for kind, a, k in rec]

    load_w(0)
    stage_QK(0)
    early_rel = []
    wo_parts = None
    for h in range(RH):
        if h + 1 < RH:
            load_w(h + 1)
        else:
            pb0 = (RH - 2) % 2
            early_rel = [wq[pb0], wk[pb0], wv[pb0], wg[pb0], qT2[pb0], kT2[pb0], ktm2[pb0]]
            B.release(*early_rel)
            wo_parts = out_proj_load_w(B, w_out, RH * RDV // 128)
        X = record(stage_CH, h)
        Y = record(stage_QK, h + 1) if h + 1 < RH else []
        i = j = 0
        while i < len(X) or j < len(Y):
            if j >= len(Y) or (i < len(X) and i * len(Y) <= j * len(X)):
                X[i]()
                i += 1
            else:
                Y[j]()
                j += 1
    B.release(*[x for x in (cos, sin, qd, kd, maskT, gain_o, hT, *wq, *wk, *wv, *wg, *qT2, *kT2, *ktm2, Tst, *Sbf2,
                            *ab[0], *ab[1], *tm, *vb, *sg, *sgg, *sTm, *ogb, *ost, ojunk) if x not in early_rel])
    if next_gain is not None:
        xacc = B.sb([128, NT, D], F32, "xacc", n=NT)
        nn = NextNorm(B, C, next_gain)
        w_pre = next_w_hook() if next_w_hook is not None else None
        out_proj_phase(B, C, og, og_t, RH * RDV // 128, w_out, xin, AccSink(B, xacc, nn), wo_parts=wo_parts)
        return xacc, nn.finish(), w_pre
    out_proj_phase(B, C, og, og_t, RH * RDV // 128, w_out, xin, DramSink(B, xout), wo_parts=wo_parts)
    return None


def bc_last(ap2d, m):
    return ap2d.unsqueeze(2).to_broadcast([ap2d.shape[0], ap2d.shape[1], m])


def dsa_prologue(B, C, pos, w_in, q_norm, k_norm, kidx_norm, K):
    import math
    CB = [(0, 512), (512, 1024), (1024, 1536), (1536, 2048), (2048, DSA_IN)]
    win = B.sb([128, KT, DSA_IN], BF16, "win", n=5)
    for i, (c0, c1) in enumerate(CB):
        B.dma("pool", win.v[:, :, c0:c1], w_in[:, c0:c1].rearrange("(k p) f -> p k f", p=128), win.t[i],
              writes=[win.t[i]])
    posi = C.posi_tm
    posf = B.sb([128, NT], F32, "posf")
    B.op("dve", lambda e: e.tensor_copy(out=posf.v, in_=posi.v), reads=[posi.t], writes=[posf.t])
    tabs = {}
    for nm, half in (("a", 64), ("i", 32)):
        invb = B.sb([128, half], F32, "invb")
        B.dma("sp", invb.v, K["inv%d" % half].partition_broadcast(128), invb.t, writes=[invb.t])
        for fn, shift in (("sin", 0.0), ("cos", math.pi / 2)):
            tab = B.sb([128, NT, half], F32, fn + nm)

            def ang(a0, invb=invb, half=half, shift=shift):
                B.op("dve", lambda e: e.tensor_tensor(out=a0.v, in0=bc_last(posf.v, half), in1=bc3(invb.v, NT),
                                                      op=ALU.mult), reads=[posf.t, invb.t], writes=[a0.t])
                if shift:
                    B.op("dve", lambda e: e.tensor_scalar(out=a0.v, in0=a0.v, scalar1=shift, scalar2=None,
                                                          op0=ALU.add), reads=[a0.t], writes=[a0.t])
            sin_table(B, tab.v, tab.t, ang, [128, NT, half])
            tabs[fn + nm] = tab
        B.release(invb)
    B.release(posf)
    gqk = B.sb([128, AH + AG, ADH], F32, "gqk")
    gq1 = B.sb([128, ADH], F32, "gq1")
    gk1 = B.sb([128, ADH], F32, "gk1")
    gki = B.sb([128, IDH], F32, "gki")
    B.dma("sp", gq1.v, q_norm.partition_broadcast(128), gq1.t, writes=[gq1.t])
    B.dma("sp", gk1.v, k_norm.partition_broadcast(128), gk1.t, writes=[gk1.t])
    B.dma("sp", gki.v, kidx_norm.partition_broadcast(128), gki.t, writes=[gki.t])
    B.op("dve", lambda e: e.tensor_scalar(out=gqk.v[:, 0:AH, :], in0=bc3(gq1.v, AH), scalar1=ADH ** -0.5,
                                          scalar2=None, op0=ALU.mult), reads=[gq1.t], writes=[gqk.t])
    B.op("dve", lambda e: e.tensor_copy(out=gqk.v[:, AH:AH + AG, :], in_=bc3(gk1.v, AG)),
         reads=[gk1.t, gqk.t], writes=[gqk.t])
    B.release(gq1, gk1)
    return {"win": win, "tabs": tabs, "gqk": gqk, "gki": gki, "CB": CB}


def dsa_phase(B, C, xin, xout, pos, attn_norm, w_in, q_norm, k_norm, kidx_norm, w_out, K, hT_pre=None, P=None, next_gain=None):
    import math
    qT = B.sb([128, NT, AH, 128], BF16, "qT", n=NT)
    kT = B.sb([128, AG, S], BF16, "kT", n=NT)
    qiT = B.sb([128, 4, S], BF16, "qiT", n=NT)
    kiT = B.sb([128, S], BF16, "kiT", n=NT)
    vtm = B.sb([128, NT, AG * ADH], BF16, "vtm", n=NT)
    wall = B.sb([128, NT, IH], F32, "wall", n=NT)
    if hT_pre is not None:
        hT = hT_pre
    else:
        hT = B.sb([128, KT, S], BF16, "hT", n=NT)
        load_norm_transpose_all(B, C, xin, attn_norm, hT)
    if P is None:
        P = dsa_prologue(B, C, pos, w_in, q_norm, k_norm, kidx_norm, K)
    win, tabs, gqk, gki, CB = P["win"], P["tabs"], P["gqk"], P["gki"], P["CB"]
    raw = [B.sb([128, DSA_IN], F32, "raw") for _ in range(2)]
    sq = B.sb([128, 1344], F32, "sq")
    st = [B.sb([128, 36], F32, "dst", n=3) for _ in range(2)]
    tmp = [B.sb([128, 640], F32, "rt") for _ in range(4)]
    qkr = [B.sb([128, AH + AG, ADH], BF16, "qkr") for _ in range(2)]
    qkir = [B.sb([128, 10, IDH], BF16, "qkir") for _ in range(2)]
    NQK = (AH + AG) * ADH
    WSC = IH ** -0.5 * IDH ** -0.5
    def st2_P(t):
            tsl = slice(t * 128, (t + 1) * 128)
            r = raw[t % 2]
            s3 = st[t % 2]
            banks = []
            for i, (c0, c1) in enumerate(CB):
                pf, pf_t = C.pf[i], C.pf_t[i]
                for k in range(KT):
                    B.op("pe", lambda e, k=k, pf=pf, c0=c0, c1=c1, tsl=tsl: e.matmul(
                        out=pf[:, 0:c1 - c0], lhsT=hT.v[:, k, tsl], rhs=win.v[:, k, c0:c1],
                        start=(k == 0), stop=(k == KT - 1)), reads=[hT.t[t], win.t[i]], writes=[pf_t])
                B.op("act", lambda e, pf=pf, c0=c0, c1=c1, r=r: e.copy(out=r.v[:, c0:c1], in_=pf[:, 0:c1 - c0]),
                     reads=[pf_t], writes=[r.t])

    def st2_N(t):
            tsl = slice(t * 128, (t + 1) * 128)
            r = raw[t % 2]
            s3 = st[t % 2]
            rqk = r.v[:, 0:NQK].rearrange("p (h d) -> p h d", h=10)
            B.op("act", lambda e, r=r: e.activation(out=sq.v[:, 0:NQK], in_=r.v[:, 0:NQK], func=AF.Square,
                                                    scale=1.0 / math.sqrt(ADH)), reads=[r.t], writes=[sq.t])
            B.op("act", lambda e, r=r: e.activation(out=sq.v[:, NQK:NQK + 64], in_=r.v[:, 2048:2112], func=AF.Square,
                                                    scale=1.0 / math.sqrt(IDH)), reads=[r.t], writes=[sq.t])
            B.op("dve", lambda e, s3=s3: e.tensor_reduce(out=s3.v[:, 0:10], in_=sq.v[:, 0:NQK].rearrange(
                "p (h d) -> p h d", h=10), axis=AX.X, op=ALU.add), reads=[sq.t], writes=[s3.t[0]])
            B.op("dve", lambda e, s3=s3: e.tensor_reduce(out=s3.v[:, 10:11], in_=sq.v[:, NQK:NQK + 64],
                                                         axis=AX.X, op=ALU.add), reads=[sq.t, s3.t[0]], writes=[s3.t[0]])
            B.op("act", lambda e, s3=s3: e.activation(out=s3.v[:, 12:23], in_=s3.v[:, 0:11], func=AF.Sqrt,
                                                      bias=C.eps.v[:, 0:1], scale=1.0),
                 reads=[s3.t[0], C.eps.t], writes=[s3.t[1]])
            B.op("dve", lambda e, s3=s3: e.reciprocal(out=s3.v[:, 24:35], in_=s3.v[:, 12:23]),
                 reads=[s3.t[1]], writes=[s3.t[2]])
            rqk = r.v[:, 0:NQK].rearrange("p (h d) -> p h d", h=10)
            B.op("dve", lambda e, rqk=rqk, s3=s3: e.tensor_tensor(out=rqk, in0=rqk, in1=bc_last(s3.v[:, 24:34], ADH),
                                                                  op=ALU.mult), reads=[r.t, s3.t[2]], writes=[r.t])
            B.op("dve", lambda e, rqk=rqk: e.tensor_tensor(out=rqk, in0=rqk, in1=gqk.v, op=ALU.mult),
                 reads=[r.t, gqk.t], writes=[r.t])
            B.op("dve", lambda e, r=r, s3=s3: e.scalar_tensor_tensor(
                out=r.v[:, 2048:2112], in0=r.v[:, 2048:2112], scalar=s3.v[:, 34:35], in1=gki.v,
                op0=ALU.mult, op1=ALU.mult), reads=[r.t, s3.t[2], gki.t], writes=[r.t])
            B.op("dve", lambda e, r=r, t=t: e.tensor_scalar(out=wall.v[:, t, :], in0=r.v[:, 2112:2120], scalar1=WSC,
                                                            scalar2=None, op0=ALU.mult), reads=[r.t], writes=[wall.t[t]])
            B.op("act", lambda e, r=r, t=t: e.copy(out=vtm.v[:, t, :], in_=r.v[:, 1280:1536]),
                 reads=[r.t], writes=[vtm.t[t]])

    def st2_R(t):
            tsl = slice(t * 128, (t + 1) * 128)
            r = raw[t % 2]
            s3 = st[t % 2]
            rqk = r.v[:, 0:NQK].rearrange("p (h d) -> p h d", h=10)
            for (x3, nh, half, cosT, sinT, dst) in (
                    (rqk, 10, 64, tabs["cosa"], tabs["sina"], qkr[t % 2]),
                    (r.v[:, 1536:2112].rearrange("p (h d) -> p h d", h=9), 9, 32, tabs["cosi"], tabs["sini"],
                     qkir[t % 2])):
                x1, x2 = x3[:, :, 0:half], x3[:, :, half:2 * half]
                cb_, sb_ = bc3(cosT.v[:, t, :], nh), bc3(sinT.v[:, t, :], nh)
                tv = [tmp[i].v[:, 0:nh * half].rearrange("p (h d) -> p h d", h=nh) for i in range(4)]
                B.op("dve", lambda e, x1=x1, cb_=cb_, tv=tv: e.tensor_tensor(out=tv[0], in0=x1, in1=cb_, op=ALU.mult),
                     reads=[r.t, cosT.t], writes=[tmp[0].t])
                B.op("pool", lambda e, x2=x2, sb_=sb_, tv=tv: e.tensor_tensor(out=tv[1], in0=x2, in1=sb_, op=ALU.mult),
                     reads=[r.t, sinT.t], writes=[tmp[1].t])
                B.op("dve", lambda e, x2=x2, cb_=cb_, tv=tv: e.tensor_tensor(out=tv[2], in0=x2, in1=cb_, op=ALU.mult),
                     reads=[r.t, cosT.t], writes=[tmp[2].t])
                B.op("pool", lambda e, x1=x1, sb_=sb_, tv=tv: e.tensor_tensor(out=tv[3], in0=x1, in1=sb_, op=ALU.mult),
                     reads=[r.t, sinT.t], writes=[tmp[3].t])
                B.op("dve", lambda e, dst=dst, nh=nh, half=half, tv=tv: e.tensor_tensor(
                    out=dst.v[:, 0:nh, 0:half], in0=tv[0], in1=tv[1], op=ALU.subtract),
                    reads=[tmp[0].t, tmp[1].t], writes=[dst.t])
                B.op("pool", lambda e, dst=dst, nh=nh, half=half, tv=tv: e.tensor_tensor(
                    out=dst.v[:, 0:nh, half:2 * half], in0=tv[2], in1=tv[3], op=ALU.add),
                    reads=[tmp[2].t, tmp[3].t, dst.t], writes=[dst.t])
            qr, qir = qkr[t % 2], qkir[t % 2]
            B.op("pool", lambda e, qir=qir: e.tensor_copy(out=qir.v[:, 9, :], in_=qir.v[:, 8, :]),
                 reads=[qir.t], writes=[qir.t])
            pb0, pb0_t = C.pb[0], C.pb_t[0]
            pb1, pb1_t = C.pb[1], C.pb_t[1]
            for h in range(AH):
                B.op("pe", lambda e, h=h, qr=qr, pb0=pb0: e.transpose(out=pb0[:, h * 128:(h + 1) * 128], in_=qr.v[:, h, :],
                                                                       identity=C.ident.v),
                     reads=[qr.t, C.ident.t], writes=[pb0_t])
            B.op("act", lambda e, t=t, pb0=pb0: e.copy(out=qT.v[:, t, :, :], in_=pb0[:].rearrange("p (h q) -> p h q", h=AH)),
                 reads=[pb0_t], writes=[qT.t[t]])
            for g in range(AG):
                B.op("pe", lambda e, g=g, qr=qr, pb1=pb1: e.transpose(out=pb1[:, g * 128:(g + 1) * 128],
                                                                       in_=qr.v[:, AH + g, :], identity=C.ident.v),
                     reads=[qr.t, C.ident.t], writes=[pb1_t])
            for p in range(5):
                B.op("pe", lambda e, p=p, qir=qir, pb1=pb1: e.transpose(
                    out=pb1[:, 256 + p * 128:256 + (p + 1) * 128],
                    in_=qir.v[:, 2 * p:2 * p + 2, :].rearrange("p a d -> p (a d)"), identity=C.ident.v),
                    reads=[qir.t, C.ident.t], writes=[pb1_t])
            B.op("act", lambda e, tsl=tsl, pb1=pb1: e.copy(out=kT.v[:, :, tsl], in_=pb1[:, 0:256].rearrange(
                "p (g s) -> p g s", g=AG)), reads=[pb1_t], writes=[kT.t[t]])
            B.op("act", lambda e, tsl=tsl, pb1=pb1: e.copy(out=qiT.v[:, :, tsl], in_=pb1[:, 256:768].rearrange(
                "p (a s) -> p a s", a=4)), reads=[pb1_t], writes=[qiT.t[t]])
            B.op("act", lambda e, tsl=tsl, pb1=pb1: e.copy(out=kiT.v[:, tsl], in_=pb1[:, 768:896]),
                 reads=[pb1_t], writes=[kiT.t[t]])

    class _Rec:
        def __init__(self):
            self.items = []

    def _record(fn, t):
        rec = []
        orig_op, orig_dma = B.op, B.dma
        B.op = lambda *a, **k: rec.append(("op", a, k))
        B.dma = lambda *a, **k: rec.append(("dma", a, k))
        try:
            fn(t)
        finally:
            B.op, B.dma = orig_op, orig_dma
        return [(lambda kind=kind, a=a, k=k: (orig_op if kind == "op" else orig_dma)(*a, **k)) for kind, a, k in rec]

    def _interleave(X, Y):
        out = []
        i = j = 0
        while i < len(X) or j < len(Y):
            if j >= len(Y) or (i < len(X) and i * len(Y) <= j * len(X)):
                out.append(X[i])
                i += 1
            else:
                out.append(Y[j])
                j += 1
        return out

    st2_P(0)
    st2_P(1)
    st2_N(0)
    for t in range(NT):
        X = _record(st2_R, t)
        Y = _record(st2_N, t + 1) if t + 1 < NT else []
        for u in _interleave(X, Y):
            u()
        if t + 2 < NT:
            st2_P(t + 2)
    B.release(hT, win, gqk, gki, sq, *raw, *st, *tmp, *qkr, *qkir, *tabs.values())

    NEG = -30000.0
    wo = B.sb([128, AH, D], BF16, "wo")
    B.dma("pool", wo.v, w_out.rearrange("(j p) n -> p j n", p=128), wo.t, writes=[wo.t])
    negm = B.sb([128, 128], F32, "negm")
    pw2 = B.sb([128, NBIS + 1], F32, "pw2")
    ones = B.sb([128, 128], BF16, "ones")
    B.dma("sp", negm.v, K["negm"], negm.t, writes=[negm.t])
    B.dma("sp", pw2.v, K["pw2"].partition_broadcast(128), pw2.t, writes=[pw2.t])
    B.op("pool", lambda e: e.memset(ones.v, 1.0), writes=[ones.t])
    tau0 = B.sb([128, 1], F32, "tau0")
    B.op("pool", lambda e: e.memset(tau0.v, -1e29), writes=[tau0.t])
    score = [B.sb([128, S], F32, "score", n=4) for _ in range(4)]
    rl = [B.sb([128, 512], F32, "rl") for _ in range(3)]
    junk_d = B.sb([128, S], BF16, "junkd")
    junk_a = B.sb([128, S], BF16, "junka")
    nmask = [B.sb([128, S], BF16, "nmask") for _ in range(2)]
    nmT = [B.sb([128, NT, 128], BF16, "nmT", n=2) for _ in range(4)]
    WO = 8
    NWO = WO + NBIS + 1
    CBC = NWO + NBIS + 1
    bst = [B.sb([128, CBC + 1], F32, "bst", n=10) for _ in range(2)]
    ebuf = [B.sb([128, 512], BF16, "ebuf") for _ in range(4)]
    rden = B.sb([128, 512], F32, "rden")
    lnd = B.sb([128, 512], F32, "lnd")
    aoT = [B.sb([128, AH, 128], BF16, "aoT", n=2) for _ in range(2)]
    xb = [B.sb([128, D], F32, "xb") for _ in range(2)]
    rot = {"i": 0, "rl": 0, "e": 0}
    nnx = {"nn": None}

    def next_bank():
        i = rot["i"]
        rot["i"] = (i + 1) % 4
        return C.pf[i], C.pf_t[i]

    def stage_AB(qt, part):
        units = []
        qsl = slice(qt * 128, (qt + 1) * 128)
        nk = qt + 1
        nkeys = nk * 128
        sc_, nm, nT, bs = score[qt % 4], nmask[qt % 2], nmT[qt % 4], bst[qt % 2]
        nkb = (nkeys + 511) // 512
        sct = sc_.t[0:nkb]
        use_act = (qt % 2 == 1)

        def idx_unit(kb, hi):
            def f():
                k0, k1 = kb * 512, min(nkeys, (kb + 1) * 512)
                po = 64 * (hi % 2)
                pf, pf_t = next_bank()
                B.op("pe", lambda e: e.matmul(
                    out=pf[:, 0:k1 - k0], lhsT=qiT.v[po:po + 64, hi // 2, qsl], rhs=kiT.v[po:po + 64, k0:k1],
                    start=True, stop=True),
                    reads=[qiT.t[qt]] + kiT.t[k0 // 128:(k1 + 127) // 128], writes=[pf_t])
                if hi == 0:
                    B.op("dve", lambda e: e.tensor_scalar(
                        out=sc_.v[:, k0:k1], in0=pf[:, 0:k1 - k0], scalar1=0.0, scalar2=wall.v[:, qt, 0:1],
                        op0=ALU.max, op1=ALU.mult), reads=[pf_t, wall.t[qt]], writes=[sc_.t[kb]])
                else:
                    rr = rl[rot["rl"]]
                    rot["rl"] = (rot["rl"] + 1) % 3
                    B.op("dve", lambda e: e.tensor_scalar(
                        out=rr.v[:, 0:k1 - k0], in0=pf[:, 0:k1 - k0], scalar1=0.0, scalar2=wall.v[:, qt, hi:hi + 1],
                        op0=ALU.max, op1=ALU.mult), reads=[pf_t, wall.t[qt]], writes=[rr.t])
                    B.op("pool", lambda e: e.tensor_tensor(
                        out=sc_.v[:, k0:k1], in0=sc_.v[:, k0:k1], in1=rr.v[:, 0:k1 - k0], op=ALU.add),
                        reads=[rr.t, sc_.t[kb]], writes=[sc_.t[kb]])
            f.w = 1.0
            return f
        if part == "A":
            for kb in range(nkb):
                for hi in range(IH):
                    units.append(idx_unit(kb, hi))
            return units

        dkb = (qt * 128) // 512

        def prelude():
            if qt >= 2:
                B.op("dve", lambda e: e.tensor_reduce(out=bs.v[:, 0:1], in_=sc_.v[:, 0:qt * 128], axis=AX.X,
                                                      op=ALU.min), reads=sct, writes=[bs.t[0]])
            B.op("dve", lambda e: e.tensor_tensor(out=sc_.v[:, qsl], in0=sc_.v[:, qsl], in1=negm.v, op=ALU.add),
                 reads=[sc_.t[dkb], negm.t], writes=[sc_.t[dkb]])
            if qt < 2:
                return
            B.op("dve", lambda e: e.tensor_reduce(out=bs.v[:, 1:2], in_=sc_.v[:, 0:nkeys], axis=AX.X, op=ALU.max),
                 reads=sct, writes=[bs.t[1]])
            B.op("dve", lambda e: e.tensor_tensor(out=bs.v[:, 2:3], in0=bs.v[:, 1:2], in1=bs.v[:, 0:1],
                                                  op=ALU.subtract), reads=[bs.t[0], bs.t[1]], writes=[bs.t[2]])
            B.op("dve", lambda e: e.tensor_scalar(out=bs.v[:, WO:WO + NBIS + 1], in0=pw2.v, scalar1=bs.v[:, 2:3],
                                                  scalar2=None, op0=ALU.mult),
                 reads=[bs.t[2], pw2.t], writes=[bs.t[3]])
            if use_act:
                B.op("dve", lambda e: e.tensor_scalar(out=bs.v[:, NWO:NWO + NBIS + 1], in0=bs.v[:, WO:WO + NBIS + 1],
                                                      scalar1=-1.0, scalar2=None, op0=ALU.mult),
                     reads=[bs.t[3]], writes=[bs.t[3]])
                B.op("dve", lambda e: e.tensor_scalar(out=bs.v[:, 3:4], in0=bs.v[:, 0:1], scalar1=-1.0,
                                                      scalar2=bs.v[:, WO:WO + 1], op0=ALU.mult, op1=ALU.subtract),
                     reads=[bs.t[0], bs.t[3]], writes=[bs.t[4]])
                B.op("pool", lambda e: e.memset(bs.v[:, CBC:CBC + 1], float(nkeys) - 2.0 * TOPK + 0.5),
                     writes=[bs.t[9]])
            else:
                B.op("dve", lambda e: e.tensor_tensor(out=bs.v[:, 3:4], in0=bs.v[:, 0:1], in1=bs.v[:, WO:WO + 1],
                                                      op=ALU.add), reads=[bs.t[0], bs.t[3]], writes=[bs.t[4]])
        prelude.w = 6.0
        units.append(prelude)

        def it_dve(k):
            def f():
                B.op("dve", lambda e: e.tensor_scalar(
                    out=junk_d.v[:, 0:nkeys], in0=sc_.v[:, 0:nkeys], scalar1=bs.v[:, 3:4], scalar2=None,
                    op0=ALU.is_ge, op1=ALU.add, accum_out=bs.v[:, 4:5]),
                    reads=sct + [bs.t[4]], writes=[junk_d.t, bs.t[5]])
                B.op("dve", lambda e: e.tensor_scalar(
                    out=bs.v[:, 5:6], in0=bs.v[:, 4:5], scalar1=float(TOPK), scalar2=bs.v[:, WO + k:WO + k + 1],
                    op0=ALU.is_ge, op1=ALU.mult), reads=[bs.t[5], bs.t[3]], writes=[bs.t[6]])
                B.op("dve", lambda e: e.scalar_tensor_tensor(
                    out=bs.v[:, 3:4], in0=bs.v[:, 5:6], scalar=bs.v[:, WO + k + 1:WO + k + 2], in1=bs.v[:, 3:4],
                    op0=ALU.subtract, op1=ALU.add), reads=[bs.t[6], bs.t[3], bs.t[4]], writes=[bs.t[4]])
            f.w = 2.5
            return f

        def it_act(k):
            cur, cur_t = (3, 4) if k % 2 == 0 else (7, 8)
            nxt, nxt_t = (7, 8) if k % 2 == 0 else (3, 4)

            def f():
                B.op("act", lambda e: e.activation(
                    out=junk_a.v[:, 0:nkeys], in_=sc_.v[:, 0:nkeys], func=AF.Sign, bias=bs.v[:, cur:cur + 1],
                    scale=1.0, accum_out=bs.v[:, 4:5]),
                    reads=sct + [bs.t[cur_t]], writes=[junk_a.t, bs.t[5]])
                B.op("act", lambda e: e.activation(out=bs.v[:, 5:6], in_=bs.v[:, 4:5], func=AF.Sign,
                                                   bias=bs.v[:, CBC:CBC + 1], scale=1.0),
                     reads=[bs.t[5], bs.t[9]], writes=[bs.t[6]])
                B.op("act", lambda e: e.activation(out=bs.v[:, nxt:nxt + 1], in_=bs.v[:, 5:6], func=AF.Identity,
                                                   bias=bs.v[:, cur:cur + 1],
                                                   scale=bs.v[:, NWO + k + 1:NWO + k + 2]),
                     reads=[bs.t[6], bs.t[3], bs.t[cur_t]], writes=[bs.t[nxt_t]])
            f.w = 2.5
            return f
        if qt >= 2:
            for k in range(NBIS):
                units.append(it_act(k) if use_act else it_dve(k))

        def finish():
            if qt >= 2:
                if use_act:
                    fin, fin_t = (3, 4) if NBIS % 2 == 0 else (7, 8)
                    B.op("dve", lambda e: e.tensor_scalar(out=bs.v[:, 6:7], in0=bs.v[:, fin:fin + 1], scalar1=-1.0,
                                                          scalar2=bs.v[:, WO + NBIS:WO + NBIS + 1],
                                                          op0=ALU.mult, op1=ALU.subtract),
                         reads=[bs.t[fin_t], bs.t[3]], writes=[bs.t[7]])
                else:
                    B.op("dve", lambda e: e.tensor_tensor(out=bs.v[:, 6:7], in0=bs.v[:, 3:4],
                                                          in1=bs.v[:, WO + NBIS:WO + NBIS + 1], op=ALU.subtract),
                         reads=[bs.t[4], bs.t[3]], writes=[bs.t[7]])
                tau_ap, tau_t = bs.v[:, 6:7], bs.t[7]
            else:
                tau_ap, tau_t = tau0.v[:, 0:1], tau0.t
            B.op("dve", lambda e: e.tensor_scalar(
                out=nm.v[:, 0:nkeys], in0=sc_.v[:, 0:nkeys], scalar1=tau_ap, scalar2=NEG, op0=ALU.is_lt,
                op1=ALU.mult), reads=sct + [tau_t], writes=[nm.t])
            for k8 in range(0, nk, 8):
                n8 = min(nk, k8 + 8) - k8
                pb, pb_t = C.next_pb()
                for j in range(n8):
                    B.op("pe", lambda e, j=j, k8=k8, pb=pb: e.transpose(
                        out=pb[:, j * 128:(j + 1) * 128], in_=nm.v[:, (k8 + j) * 128:(k8 + j + 1) * 128],
                        identity=C.ident.v), reads=[nm.t, C.ident.t], writes=[pb_t])
                B.op("act", lambda e, k8=k8, n8=n8, pb=pb: e.copy(
                    out=nT.v[:, k8:k8 + n8, :], in_=pb[:, 0:n8 * 128].rearrange("p (k q) -> p k q", k=n8)),
                    reads=[pb_t], writes=[nT.t[k8 // 8]])
        finish.w = 4.0
        units.append(finish)
        return units

    def stage_C(qt):
        units = []
        nk = qt + 1
        nT = nmT[qt % 4]
        ao = aoT[qt % 2]

        def L_unit(g, kt, eb):
            def f():
                ksl = slice(kt * 128, (kt + 1) * 128)
                pf, pf_t = next_bank()
                B.op("pe", lambda e: e.matmul(
                    out=pf[:], lhsT=kT.v[:, g, ksl],
                    rhs=qT.v[:, qt, 4 * g:4 * g + 4, :].rearrange("p h q -> p (h q)"), start=True, stop=False),
                    reads=[kT.t[kt], qT.t[qt]], writes=[pf_t])
                B.op("pe", lambda e: e.matmul(
                    out=pf[:].rearrange("p (h q) -> p h q", h=4), lhsT=C.ident.v, rhs=bc3(nT.v[:, kt, :], 4),
                    start=False, stop=True), reads=[C.ident.t, nT.t[kt // 8]], writes=[pf_t])
                B.op("act", lambda e: e.activation(out=eb.v, in_=pf[:], func=AF.Exp), reads=[pf_t], writes=[eb.t])
            return f

        def P_unit(g, kt, eb):
            def f():
                B.op("pe", lambda e: e.matmul(
                    out=C.pf[4][:], lhsT=vtm.v[:, kt, g * ADH:(g + 1) * ADH], rhs=eb.v,
                    start=(kt == 0), stop=(kt == nk - 1)), reads=[vtm.t[kt], eb.t], writes=[C.pf_t[4]])
                B.op("pe", lambda e: e.matmul(
                    out=C.pf[5][:], lhsT=ones.v, rhs=eb.v, start=(kt == 0), stop=(kt == nk - 1)),
                    reads=[ones.t, eb.t], writes=[C.pf_t[5]])
            f.w = 0.5
            return f

        def evac_unit(g):
            def f():
                B.op("act", lambda e: e.activation(out=lnd.v, in_=C.pf[5][:], func=AF.Ln), reads=[C.pf_t[5]],
                     writes=[lnd.t])
                B.op("act", lambda e: e.activation(out=rden.v, in_=lnd.v, func=AF.Exp, scale=-1.0), reads=[lnd.t],
                     writes=[rden.t])
                B.op("dve", lambda e: e.tensor_tensor(
                    out=ao.v[:, 4 * g:4 * g + 4, :].rearrange("p h q -> p (h q)"), in0=C.pf[4][:], in1=rden.v,
                    op=ALU.mult), reads=[C.pf_t[4], rden.t], writes=[ao.t[g]])
            f.w = 2.0
            return f
        LAG = 2
        seq = [(g, kt) for g in range(AG) for kt in range(nk)]
        ebs = []
        for i in range(len(seq) + LAG):
            if i < len(seq):
                eb = ebuf[rot["e"]]
                rot["e"] = (rot["e"] + 1) % len(ebuf)
                ebs.append(eb)
                units.append(L_unit(seq[i][0], seq[i][1], eb))
            j = i - LAG
            if j >= 0:
                g, kt = seq[j]
                units.append(P_unit(g, kt, ebs[j]))
                if kt == nk - 1:
                    units.append(evac_unit(g))

        def outproj():
            xbt = xb[qt % 2]
            B.dma("sp", xbt.v, xin.tile(qt), xbt.t, reads=[xin.t[qt]], writes=[xbt.t])
            for half in range(2):
                pf, pf_t = next_bank()
                for h in range(AH):
                    B.op("pe", lambda e, h=h, half=half, pf=pf: e.matmul(
                        out=pf[:], lhsT=ao.v[:, h, :], rhs=wo.v[:, h, half * 512:(half + 1) * 512],
                        start=(h == 0), stop=(h == AH - 1)), reads=[ao.t[h // 4], wo.t], writes=[pf_t])
                B.op("dve", lambda e, half=half, pf=pf: e.tensor_tensor(
                    out=xbt.v[:, half * 512:(half + 1) * 512], in0=xbt.v[:, half * 512:(half + 1) * 512], in1=pf[:],
                    op=ALU.add), reads=[pf_t, xbt.t], writes=[xbt.t])
            B.dma("sp", xout.tile(qt), xbt.v, xbt.t, reads=[xbt.t], writes=[xout.t[qt]])
            if nnx["nn"] is not None:
                nnx["nn"].feed(qt, xbt.v, xbt.t)
        outproj.w = 4.0
        units.append(outproj)
        return units

    def merge(*streams):
        streams = [s for s in streams if s]
        tot = [sum(getattr(u, "w", 1.0) for u in s) for s in streams]
        done = [0.0] * len(streams)
        idx = [0] * len(streams)
        out = []
        while True:
            best, bf = None, None
            for k, s in enumerate(streams):
                if idx[k] < len(s):
                    fr = done[k] / tot[k]
                    if bf is None or fr < bf:
                        best, bf = k, fr
            if best is None:
                break
            u = streams[best][idx[best]]
            out.append(u)
            done[best] += getattr(u, "w", 1.0)
            idx[best] += 1
        return out

    def zip_units(a, b):
        out = []
        for i in range(max(len(a), len(b))):
            if i < len(a):
                out.append(a[i])
            if i < len(b):
                out.append(b[i])
        return out

    for u in stage_AB(0, "A") + stage_AB(1, "A"):
        u()
    NP = NT // 2
    early = []
    for m in range(NP + 1):
        X = (stage_AB(2 * m + 2, "A") + stage_AB(2 * m + 3, "A")) if m + 1 < NP else []
        Bm = zip_units(stage_AB(2 * m, "B"), stage_AB(2 * m + 1, "B")) if m < NP else []
        Cm = (stage_C(2 * m - 2) + stage_C(2 * m - 1)) if m >= 1 else []
        Z = []
        if m == NP and next_gain is not None:
            early = [qiT, kiT, *score, *rl, junk_d, junk_a, *nmask, *bst]
            B.release(*early)
            nnx["nn"] = NextNorm(B, C, next_gain)
            xz = [B.sb([128, D], F32, "xz") for _ in range(2)]

            def z_unit(t):
                def f():
                    xt_ = xz[t % 2]
                    B.dma("sp", xt_.v, xout.tile(t), xt_.t, reads=[xout.t[t]], writes=[xt_.t])
                    nnx["nn"].feed(t, xt_.v, xt_.t)
                return f
            Z = [z_unit(t) for t in range(NT - 2)]
        for u in merge(X, Bm, Cm, Z):
            u()
    hT_next = None
    if nnx["nn"] is not None:
        hT_next = nnx["nn"].finish()
        B.release(*xz)
    B.release(*[x for x in (qT, kT, qiT, kiT, vtm, wall, wo, negm, pw2, ones, tau0, *score, *rl, junk_d, junk_a,
                            *nmask, *nmT, *bst, *ebuf, rden, lnd, *aoT, *xb) if x not in early])
    return hT_next


ALL_INPUTS = [("x", [S, D], F32), ("positions", [S], I32), ("attn_norm", [2, D], F32),
              ("ret_w_in", [1, D, RET_IN], F32), ("ret_out_norm", [1, RH, RDV], F32),
              ("ret_w_out", [1, RH * RDV, D], F32), ("dsa_w_in", [1, D, DSA_IN], F32),
              ("dsa_q_norm", [1, ADH], F32), ("dsa_k_norm", [1, ADH], F32), ("dsa_kidx_norm", [1, IDH], F32),
              ("dsa_w_out", [1, AH * ADH, D], F32), ("mlp_norm", [2, D], F32),
              ("mlp_w_up", [2, D, DFF], F32), ("mlp_w_down", [2, DFF, D], F32)]


def build(phases=("ret", "mlp0", "dsa", "mlp1")):
    nc = bass.Bass("TRN2", target_bir_lowering=False)
    es = ExitStack()
    with es:
        B = Builder(nc, es)
        dt = nc.dram_tensor
        I = {name: dt(name, shape, dty, kind="ExternalInput").ap() for name, shape, dty in ALL_INPUTS}
        K = {name: dt(name, list(arr.shape), BF16 if arr.dtype != np.float32 else F32, kind="ExternalInput").ap()
             for name, arr in _consts().items()}
        out = dt("out", [S, D], F32, kind="ExternalOutput").ap()
        C = Common(B, K)
        C.posi_tm = B.sb([128, NT], I32, "posi_tm")

        def load_posi_tm():
            for t in range(NT):
                B.sc.dma("sp", lambda e, t=t: e.dma_start(
                    out=C.posi_tm.v[:, t:t + 1],
                    in_=I["positions"][t * 128:(t + 1) * 128].rearrange("(p o) -> p o", o=1)),
                    C.posi_tm.t, writes=[C.posi_tm.t])
        if phases[0] == "dsa":
            load_posi_tm()
        cur = DX(I["x"])
        pre = None
        w_pre_next = None
        for i, ph in enumerate(phases):
            last = i == len(phases) - 1
            nxt = phases[i + 1] if not last else None
            dst = DX(out if last else dt("xs%d" % i, [S, D], F32, kind="Internal").ap())
            if ph in ("mlp0", "mlp1"):
                l = int(ph[-1])
                hook = None
                if nxt == "dsa":
                    hook = lambda: dsa_prologue(B, C, I["positions"], I["dsa_w_in"][0], I["dsa_q_norm"][0],
                                                I["dsa_k_norm"][0], I["dsa_kidx_norm"][0], K)
                pre = mlp_phase(B, C, cur, dst, I["mlp_norm"][l], I["mlp_w_up"][l], I["mlp_w_down"][l], pre=pre,
                                next_gain=I["attn_norm"][1] if nxt == "dsa" else None, tail_hook=hook,
                                w_pre=w_pre_next)
                w_pre_next = None
                if nxt != "dsa":
                    pre = None
            elif ph == "ret":
                ng = I["mlp_norm"][int(nxt[-1])] if nxt in ("mlp0", "mlp1") else None
                hk = None
                if ng is not None:
                    lr = int(nxt[-1])
                    hk = lambda: mlp_prefetch_w0(B, I["mlp_w_up"][lr], I["mlp_w_down"][lr])
                pre = ret_phase(B, C, cur, dst, I["positions"], I["attn_norm"][0], I["ret_w_in"][0],
                                I["ret_out_norm"][0], I["ret_w_out"][0], K, next_gain=ng, next_w_hook=hk)
                if pre is not None:
                    w_pre_next = pre[2]
                    pre = (pre[0], pre[1])
            elif ph == "dsa":
                ng = I["mlp_norm"][int(nxt[-1])] if nxt in ("mlp0", "mlp1") else None
                hTn = dsa_phase(B, C, cur, dst, I["positions"], I["attn_norm"][1], I["dsa_w_in"][0],
                                I["dsa_q_norm"][0], I["dsa_k_norm"][0], I["dsa_kidx_norm"][0], I["dsa_w_out"][0], K,
                                hT_pre=pre[0] if pre else None, P=pre[1] if pre else None, next_gain=ng)
                pre = (None, hTn) if hTn is not None else None
            else:
                raise ValueError(ph)
            cur = dst
            if i == 0 and phases[0] != "dsa":
                load_posi_tm()
        B.sc.wait_all("sp", cur.t)
        B.sc.emit()
        print("SBUF peak bytes/partition:", B.peak, "ops:", {k: len(v) for k, v in B.sc.ops.items()},
              "dsems:", B.sc.ndsem)
    return nc


_CONSTS = None


def _consts():
    global _CONSTS
    if _CONSTS is None:
        import ml_dtypes
        bf = ml_dtypes.bfloat16
        c = {}
        c["ident"] = np.eye(128, dtype=np.float32).astype(bf)
        c["inv128"] = (10000.0 ** (-np.arange(128, dtype=np.float64) / 128)).astype(np.float32).reshape(128, 1)
        i = np.arange(128, dtype=np.float64)
        gam = 1.0 - 2.0 ** (-5.0 - np.arange(RH, dtype=np.float64))
        c["qdec"] = (gam[:, None] ** (i[None, :] + 1.0)).astype(np.float32).reshape(-1)
        c["kdec"] = (gam[:, None] ** (-(i[None, :] + 1.0)) * RDK ** -0.5).astype(np.float32).reshape(-1)
        c["maskT"] = (i[None, :] >= i[:, None]).astype(np.float32).astype(bf)
        c["inv64"] = (10000.0 ** (-np.arange(64, dtype=np.float64) / 64)).astype(np.float32)
        c["inv32"] = (10000.0 ** (-np.arange(32, dtype=np.float64) / 32)).astype(np.float32)
        c["negm"] = np.where(i[None, :] > i[:, None], -1e30, 0.0).astype(np.float32)
        c["pw2"] = (2.0 ** -(np.arange(NBIS + 1, dtype=np.float64) + 1.0)).astype(np.float32)
        _CONSTS = c
    return _CONSTS


def run(phases, inputs, x_override=None):
    nc = build(phases)
    c = _consts()
    in_maps = []
    xs = inputs["x"] if x_override is None else x_override
    for b in range(NCORES):
        m = {}
        for name, shape, dty in ALL_INPUTS:
            if name == "x":
                m[name] = np.ascontiguousarray(xs[b])
            elif name == "positions":
                m[name] = np.ascontiguousarray(inputs[name][b]).astype(np.int32)
            else:
                m[name] = np.ascontiguousarray(inputs[name])
        m.update(c)
        in_maps.append(m)
    res = run_bass_kernel_spmd(nc, in_maps, core_ids=list(range(NCORES)))
    if DEBUG:
        LAST["res"] = res.results
    return np.stack([np.asarray(r["out"]) for r in res.results], axis=0)


def kernel(**inputs):
    inputs = {k: np.asarray(v) for k, v in inputs.items()}
    return run(("ret", "mlp0", "dsa", "mlp1"), inputs)
```

```python
import numpy as np
from contextlib import ExitStack

import concourse.bass as bass
import concourse.mybir as mybir
from concourse.bass_utils import run_bass_kernel_spmd

F32 = mybir.dt.float32
BF16 = mybir.dt.bfloat16
I32 = mybir.dt.int32
ALU = mybir.AluOpType
AF = mybir.ActivationFunctionType
AX = mybir.AxisListType

NCORES = 8
S = 2048
D = 1024
NT = S // 128
KT = D // 128
DFF = 4096
EPS = 1e-6
RH, RDK, RDV = 4, 256, 512
RET_IN = 6144
AH, ADH, AG = 8, 128, 2
IH, IDH = 8, 64
DSA_IN = 2120
TOPK = 256
NBIS = 16
DEBUG = False
LAST = {}


class Trk:
    __slots__ = ("name", "w", "r", "dsem")

    def __init__(self, name=""):
        self.name = name
        self.w = None
        self.r = {}
        self.dsem = None


class DSem:
    __slots__ = ("h", "count", "q")

    def __init__(self, h, q):
        self.h = h
        self.count = 0
        self.q = q


class Sched:
    COMPUTE = ("pe", "act", "dve", "pool")
    ALL = ("pe", "act", "dve", "pool", "sp")

    def __init__(self, nc, es):
        self.nc = nc
        self.es = es
        self.ops = {e: [] for e in self.ALL}
        self.count = {e: 0 for e in self.COMPUTE}
        self.sem = {e: es.enter_context(nc.semaphore("sem_" + e)) for e in self.COMPUTE}
        self.seen = {e: {} for e in self.ALL}
        self.ndsem = 0

    def new_dsem(self, q):
        self.ndsem += 1
        return DSem(self.es.enter_context(self.nc.semaphore("dsem%d" % self.ndsem)), q)

    def _semh(self, key):
        return self.sem[key] if isinstance(key, str) else key.h

    def _collect(self, eng, reads, writes, is_dma=False):
        deps = {}

        def add(ev, raw):
            if ev is None:
                return
            key, val = ev
            if isinstance(key, str) and key == eng and not is_dma:
                if eng == "pe":
                    return
            if deps.get(key, 0) < val:
                deps[key] = val

        for t in reads:
            add(t.w, True)
        for t in writes:
            add(t.w, False)
            for k, v in t.r.items():
                add((k, v), False)
        waits = []
        seen = self.seen[eng]
        for key, val in deps.items():
            if seen.get(key, 0) < val:
                seen[key] = val
                waits.append((key, val))
        return waits

    def op(self, eng, fn, reads=(), writes=()):
        waits = self._collect(eng, reads, writes)
        self.count[eng] += 1
        ev = (eng, self.count[eng])
        for t in reads:
            if t.r.get(eng, 0) < ev[1]:
                t.r[eng] = ev[1]
        for t in writes:
            t.w = ev
            t.r = {}
        self.ops[eng].append((waits, fn, ("c", eng, ev[1])))

    def dma(self, eng, fn, owner, reads=(), writes=()):
        waits = self._collect(eng, reads, writes, is_dma=True)
        if owner.dsem is None:
            owner.dsem = self.new_dsem(eng)
        ds = owner.dsem
        assert ds.q == eng, "DMA semaphore shared between issuing queues"

        ds.count += 16
        ev = (ds, ds.count)
        for t in reads:
            if t.r.get(ds, 0) < ev[1]:
                t.r[ds] = ev[1]
        for t in writes:
            t.w = ev
            t.r = {}
        self.ops[eng].append((waits, fn, ("d", ds.h, 16)))

    def wait_all(self, eng, trks):
        waits = self._collect(eng, (), trks, is_dma=True)
        if waits:
            self.ops[eng].append((waits, None, None))

    def emit(self):
        nc = self.nc
        needed = {e: set() for e in self.COMPUTE}
        for name in self.ALL:
            for waits, fn, inc in self.ops[name]:
                for key, val in waits:
                    if isinstance(key, str):
                        needed[key].add(val)
        rank = {e: {t: i + 1 for i, t in enumerate(sorted(needed[e]))} for e in self.COMPUTE}

        def replay(name):
            def run(eng):
                for waits, fn, inc in self.ops[name]:
                    for key, val in waits:
                        if isinstance(key, str):
                            eng.wait_ge(self.sem[key], rank[key][val])
                        else:
                            eng.wait_ge(key.h, val)
                    if fn is not None:
                        ins = fn(eng)
                        if inc[0] == "d":
                            ins.then_inc(inc[1], inc[2])
                        elif inc[2] in needed[inc[1]]:
                            ins.then_inc(self.sem[inc[1]], 1)
            return run

        with nc.Block() as block:
            block.sync(replay("sp"))
            block.tensor(replay("pe"))
            block.scalar(replay("act"))
            block.vector(replay("dve"))
            block.gpsimd(replay("pool"))


ARENA = 204 * 1024
U8 = mybir.dt.uint8
_DTSIZE = {F32: 4, BF16: 2, I32: 4}


class T:
    __slots__ = ("v", "t", "off", "size")

    def __init__(self, v, t, off, size):
        self.v, self.t, self.off, self.size = v, t, off, size

    def trks(self):
        return self.t if isinstance(self.t, list) else [self.t]


class Builder:
    def __init__(self, nc, es):
        self.nc = nc
        self.es = es
        self.sc = Sched(nc, es)
        self.arena = es.enter_context(nc.sbuf_tensor("arena", [128, ARENA], U8))
        self.free = [(0, ARENA)]
        self.ghosts = []
        self.free_dsems = {"sp": [], "pool": [], "act": []}
        self.npsum = 0
        self.peak = 0

    def sb(self, shape, dtype, name="", n=1):
        elems = 1
        for s in shape[1:]:
            elems *= s
        nbytes = (elems * _DTSIZE[dtype] + 63) // 64 * 64
        off = None
        for i, (o, sz) in enumerate(self.free):
            if sz >= nbytes:
                off = o
                if sz == nbytes:
                    self.free.pop(i)
                else:
                    self.free[i] = (o + nbytes, sz - nbytes)
                break
        if off is None:
            raise RuntimeError("SBUF arena full allocating %s %s (free=%s)" % (name, shape, self.free))
        self.peak = max(self.peak, off + nbytes)
        v = self.arena[:, off:off + elems * _DTSIZE[dtype]].bitcast(dtype)
        if len(shape) == 3:
            v = v.rearrange("p (a b) -> p a b", a=shape[1])
        elif len(shape) == 4:
            v = v.rearrange("p (a b c) -> p a b c", a=shape[1], b=shape[2])
        trks = [Trk(name) for _ in range(n)]
        keep = []
        for (go, gs, ev) in self.ghosts:
            if go < off + nbytes and off < go + gs:
                for tr in trks:
                    for k, val in ev.items():
                        if tr.r.get(k, 0) < val:
                            tr.r[k] = val
                if go >= off and go + gs <= off + nbytes:
                    continue
            keep.append((go, gs, ev))
        self.ghosts = keep
        return T(v, trks if n > 1 else trks[0], off, nbytes)

    def release(self, *tiles):
        for tile in tiles:
            ev = {}
            for tr in tile.trks():
                if tr.w is not None and ev.get(tr.w[0], 0) < tr.w[1]:
                    ev[tr.w[0]] = tr.w[1]
                for k, val in tr.r.items():
                    if ev.get(k, 0) < val:
                        ev[k] = val
                if tr.dsem is not None:
                    self.free_dsems[tr.dsem.q].append(tr.dsem)
                    tr.dsem = None
            self.ghosts.append((tile.off, tile.size, ev))
            self.free.append((tile.off, tile.size))
        self.free.sort()
        merged = []
        for o, sz in self.free:
            if merged and merged[-1][0] + merged[-1][1] == o:
                merged[-1] = (merged[-1][0], merged[-1][1] + sz)
            else:
                merged.append((o, sz))
        self.free = merged

    def ps(self, shape, dtype, name=None):
        self.npsum += 1
        return self.es.enter_context(self.nc.psum_tensor("%s_%d" % (name or "ps", self.npsum), list(shape), dtype))

    def op(self, eng, fn, reads=(), writes=()):
        self.sc.op(eng, fn, reads, writes)

    def dma(self, eng, out, in_, owner, reads=(), writes=()):
        if owner.dsem is None and self.free_dsems[eng]:
            ds = self.free_dsems[eng].pop()
            owner.dsem = ds
            if ds.count:
                for tr in list(writes) + [owner]:
                    if tr.r.get(ds, 0) < ds.count:
                        tr.r[ds] = ds.count
        self.sc.dma(eng, lambda e: e.dma_start(out=out, in_=in_), owner, reads, writes)


class Common:
    def __init__(self, B, consts):
        self.B = B
        self.ident = B.sb([128, 128], BF16, "ident")
        B.dma("pool", self.ident.v, consts["ident"], self.ident.t, writes=[self.ident.t])
        self.eps = B.sb([128, 1], F32, "eps")
        B.op("pool", lambda e: e.memset(self.eps.v, EPS), writes=[self.eps.t])
        self.mhalf = B.sb([128, 16], F32, "mhalf")
        B.op("pool", lambda e: e.memset(self.mhalf.v, -0.5), writes=[self.mhalf.t])
        self.pf = [B.ps([128, 512], F32, "pf") for _ in range(6)]
        self.pf_t = [Trk("pf%d" % i) for i in range(6)]
        self.pb = [B.ps([128, 1024], BF16, "pb") for _ in range(2)]
        self.pb_t = [Trk("pb%d" % i) for i in range(2)]
        self.pf_i = 0
        self.pb_i = 0

    def next_pf(self):
        i = self.pf_i
        self.pf_i = (i + 1) % 6
        return self.pf[i], self.pf_t[i]

    def next_pb(self):
        i = self.pb_i
        self.pb_i = (i + 1) % 2
        return self.pb[i], self.pb_t[i]


class NormScratch:
    def __init__(self, B):
        self.junk = B.sb([128, D], BF16, "junk")
        self.st = B.sb([128, 4], F32, "nstat", n=3)
        self.hb = B.sb([128, D], BF16, "hb")

    def tiles(self):
        return [self.junk, self.st, self.hb]


def norm_s1(B, C, x_ap, x_trk, gain, scr):
    st = scr.st
    B.op("act", lambda e: e.activation(out=scr.junk.v, in_=x_ap, func=AF.Square,
                                       scale=1.0 / 32.0, accum_out=st.v[:, 0:1]),
         reads=[x_trk], writes=[scr.junk.t, st.t[0]])
    B.op("pool", lambda e: e.tensor_scalar(out=st.v[:, 1:1+1], in0=st.v[:, 0:0+1], scalar1=EPS,
                                           scalar2=None, op0=ALU.add), reads=[st.t[0]], writes=[st.t[1]])
    B.op("pool", lambda e: e.tensor_tensor(out=st.v[:, 2:2+1], in0=st.v[:, 1:1+1],
                                           in1=C.mhalf.v[:, 0:1], op=ALU.pow), reads=[st.t[1], C.mhalf.t], writes=[st.t[2]])
    B.op("dve", lambda e: e.scalar_tensor_tensor(out=scr.hb.v, in0=x_ap, scalar=st.v[:, 2:3],
                                                 in1=gain.v, op0=ALU.mult, op1=ALU.mult),
         reads=[x_trk, st.t[2], gain.t], writes=[scr.hb.t])


def norm_s2(B, C, t, hT, scr):
    pb, pb_t = C.next_pb()
    for k in range(KT):
        B.op("pe", lambda e, k=k: e.transpose(out=pb[:, k * 128:(k + 1) * 128],
                                              in_=scr.hb.v[:, k * 128:(k + 1) * 128], identity=C.ident.v),
             reads=[scr.hb.t, C.ident.t], writes=[pb_t])
    B.op("act", lambda e: e.copy(out=hT.v[:, :, t * 128:(t + 1) * 128],
                                 in_=pb[:].rearrange("p (k n) -> p k n", k=KT)),
         reads=[pb_t], writes=[hT.t[t]])


NLAG = 3


class DX:
    def __init__(self, ap):
        self.ap = ap
        self.t = [Trk("dx") for _ in range(NT)]

    def tile(self, t):
        return self.ap[t * 128:(t + 1) * 128, :]


def bc3(ap2d, n):
    return ap2d.unsqueeze(1).to_broadcast([ap2d.shape[0], n, ap2d.shape[1]])


def sin_table(B, dst_ap, dst_trk, ang_fn, shape):
    import math
    TWO_PI = 2 * math.pi
    a0 = B.sb(shape, F32, "a0")
    ki = B.sb(shape, I32, "ki")
    kf = B.sb(shape, F32, "kf")
    ang_fn(a0)
    B.op("dve", lambda e: e.tensor_scalar(out=ki.v, in0=a0.v, scalar1=1.0 / TWO_PI, scalar2=None, op0=ALU.mult),
         reads=[a0.t], writes=[ki.t])
    B.op("dve", lambda e: e.tensor_copy(out=kf.v, in_=ki.v), reads=[ki.t], writes=[kf.t])
    B.op("dve", lambda e: e.scalar_tensor_tensor(out=a0.v, in0=kf.v, scalar=-TWO_PI, in1=a0.v,
                                                 op0=ALU.mult, op1=ALU.add), reads=[kf.t, a0.t], writes=[a0.t])
    B.op("dve", lambda e: e.tensor_scalar(out=kf.v, in0=a0.v, scalar1=math.pi, scalar2=-TWO_PI,
                                          op0=ALU.is_gt, op1=ALU.mult), reads=[a0.t], writes=[kf.t])
    B.op("dve", lambda e: e.tensor_tensor(out=a0.v, in0=a0.v, in1=kf.v, op=ALU.add), reads=[a0.t, kf.t], writes=[a0.t])
    B.op("dve", lambda e: e.tensor_scalar(out=a0.v, in0=a0.v, scalar1=-math.pi, scalar2=math.pi,
                                          op0=ALU.max, op1=ALU.min), reads=[a0.t], writes=[a0.t])
    B.op("act", lambda e: e.activation(out=dst_ap, in_=a0.v, func=AF.Sin), reads=[a0.t], writes=[dst_trk])
    B.release(a0, ki, kf)


def load_norm_transpose_all(B, C, xin, gain_dram, hT, keep=None):
    gain = B.sb([128, D], F32, "gain")
    B.dma("sp", gain.v, gain_dram.partition_broadcast(128), gain.t, writes=[gain.t])
    scr = [NormScratch(B) for _ in range(NLAG + 1)]
    xb = None
    if keep is None:
        xb = [B.sb([128, D], F32, "xb") for _ in range(3)]
    for t in range(NT + NLAG):
        if t < NT:
            if keep is not None:
                xap, xt = keep.v[:, t, :], keep.t[t]
            else:
                xap, xt = xb[t % 3].v, xb[t % 3].t
            B.dma("sp", xap, xin.tile(t), xt, reads=[xin.t[t]], writes=[xt])
            norm_s1(B, C, xap, xt, gain, scr[t % (NLAG + 1)])
        if t - NLAG >= 0:
            norm_s2(B, C, t - NLAG, hT, scr[(t - NLAG) % (NLAG + 1)])
    for s in scr:
        B.release(*s.tiles())
    B.release(gain)
    if xb:
        B.release(*xb)


def mlp_prefetch_w0(B, w_up, w_down):
    FG = 512
    wup0 = B.sb([128, KT, FG], BF16, "wup")
    wdn0 = B.sb([128, FG // 128, D], BF16, "wdn")
    B.dma("pool", wup0.v, w_up[:, 0:FG].rearrange("(k p) f -> p k f", p=128), wup0.t, writes=[wup0.t])
    B.dma("pool", wdn0.v, w_down[0:FG, :].rearrange("(j p) n -> p j n", p=128), wdn0.t, writes=[wdn0.t])
    return wup0, wdn0


def mlp_phase(B, C, xin, xout, mlp_norm, w_up, w_down, pre=None, next_gain=None, tail_hook=None, w_pre=None):
    if pre is not None:
        xacc, hT = pre
        if xacc is None:
            xacc = B.sb([128, NT, D], F32, "xacc", n=NT)
            for t in range(NT):
                B.dma("sp", xacc.v[:, t, :], xin.tile(t), xacc.t[t], reads=[xin.t[t]], writes=[xacc.t[t]])
    else:
        xacc = B.sb([128, NT, D], F32, "xacc", n=NT)
        hT = B.sb([128, KT, S], BF16, "hT", n=NT)
        load_norm_transpose_all(B, C, xin, mlp_norm, hT, keep=xacc)
    nn = None
    hook_res = None
    released = []

    FG = 512
    NJ = FG // 128
    NG = DFF // FG
    if w_pre is not None:
        wup = [w_pre[0], B.sb([128, KT, FG], BF16, "wup")]
        wdn = [w_pre[1], B.sb([128, NJ, D], BF16, "wdn")]
    else:
        wup = [B.sb([128, KT, FG], BF16, "wup") for _ in range(2)]
        wdn = [B.sb([128, NJ, D], BF16, "wdn") for _ in range(2)]
    uT = B.sb([128, NJ, S], BF16, "uT", n=NJ * 4)
    rl = [B.sb([128, 512], F32, "rl") for _ in range(2)]
    rli = 0
    for g in range(NG):
        b = g % 2
        if not (g == 0 and w_pre is not None):
            B.dma("pool", wup[b].v, w_up[:, g * FG:(g + 1) * FG].rearrange("(k p) f -> p k f", p=128),
                  wup[b].t, writes=[wup[b].t])
            B.dma("pool", wdn[b].v, w_down[g * FG:(g + 1) * FG, :].rearrange("(j p) n -> p j n", p=128),
                  wdn[b].t, writes=[wdn[b].t])
        for j in range(NJ):
            for tb in range(4):
                pf, pf_t = C.next_pf()
                for k in range(KT):
                    B.op("pe", lambda e, k=k, j=j, tb=tb, pf=pf, b=b: e.matmul(
                        out=pf[:], lhsT=wup[b].v[:, k, j * 128:(j + 1) * 128],
                        rhs=hT.v[:, k, tb * 512:(tb + 1) * 512], start=(k == 0), stop=(k == KT - 1)),
                        reads=[wup[b].t] + hT.t[tb * 4:tb * 4 + 4], writes=[pf_t])
                r = rl[rli]
                rli ^= 1
                B.op("act", lambda e, pf=pf, r=r: e.activation(out=r.v, in_=pf[:], func=AF.Relu),
                     reads=[pf_t], writes=[r.t])
                B.op("pool", lambda e, r=r, j=j, tb=tb: e.tensor_tensor(
                    out=uT.v[:, j, tb * 512:(tb + 1) * 512], in0=r.v, in1=r.v, op=ALU.mult),
                    reads=[r.t], writes=[uT.t[j * 4 + tb]])
        if g == NG - 1 and next_gain is not None:
            released = [hT, *wup, wdn[1 - b], *rl]
            B.release(*released)
            nn = NextNorm(B, C, next_gain)
            if tail_hook is not None:
                hook_res = tail_hook()
        for t in range(NT):
            for half in range(2):
                pf, pf_t = C.next_pf()
                for j in range(NJ):
                    B.op("pe", lambda e, j=j, t=t, half=half, pf=pf, b=b: e.matmul(
                        out=pf[:], lhsT=uT.v[:, j, t * 128:(t + 1) * 128],
                        rhs=wdn[b].v[:, j, half * 512:(half + 1) * 512],
                        start=(j == 0), stop=(j == NJ - 1)),
                        reads=[wdn[b].t, uT.t[j * 4 + t // 4]], writes=[pf_t])
                B.op("dve", lambda e, t=t, half=half, pf=pf: e.tensor_tensor(
                    out=xacc.v[:, t, half * 512:(half + 1) * 512], in0=xacc.v[:, t, half * 512:(half + 1) * 512],
                    in1=pf[:], op=ALU.add),
                    reads=[pf_t, xacc.t[t]], writes=[xacc.t[t]])
            if g == NG - 1:
                B.dma("sp", xout.tile(t), xacc.v[:, t, :], xacc.t[t], reads=[xacc.t[t]], writes=[xout.t[t]])
                if nn is not None:
                    nn.feed(t, xacc.v[:, t, :], xacc.t[t])
    B.release(*[x for x in (xacc, hT, uT, *wup, *wdn, *rl) if x not in released])
    return (nn.finish() if nn is not None else None), hook_res


class NextNorm:
    def __init__(self, B, C, gain_dram):
        self.B, self.C = B, C
        self.hT = B.sb([128, KT, S], BF16, "hTn", n=NT)
        self.gain = B.sb([128, D], F32, "gainn")
        B.dma("sp", self.gain.v, gain_dram.partition_broadcast(128), self.gain.t, writes=[self.gain.t])
        self.scr = [NormScratch(B) for _ in range(NLAG + 1)]
        self.pending = []
        self.nfed = 0

    def feed(self, t, x_ap, x_trk):
        scr = self.scr[self.nfed % (NLAG + 1)]
        self.nfed += 1
        norm_s1(self.B, self.C, x_ap, x_trk, self.gain, scr)
        self.pending.append((t, scr))
        if len(self.pending) > NLAG:
            t2, s2 = self.pending.pop(0)
            norm_s2(self.B, self.C, t2, self.hT, s2)

    def finish(self):
        for t2, s2 in self.pending:
            norm_s2(self.B, self.C, t2, self.hT, s2)
        self.pending = []
        for s in self.scr:
            self.B.release(*s.tiles())
        self.B.release(self.gain)
        return self.hT


class DramSink:
    def __init__(self, B, xout, nn=None, nbuf=2):
        self.B, self.xout, self.nn = B, xout, nn
        self.xb = [B.sb([128, D], F32, "xb") for _ in range(nbuf)]

    def xtile(self, t):
        x = self.xb[t % len(self.xb)]
        return x.v, x.t

    def done(self, t):
        ap, trk = self.xtile(t)
        self.B.dma("sp", self.xout.tile(t), ap, trk, reads=[trk], writes=[self.xout.t[t]])
        if self.nn is not None:
            self.nn.feed(t, ap, trk)

    def finish(self):
        self.B.release(*self.xb)


class AccSink:
    def __init__(self, B, xacc, nn):
        self.B, self.xacc, self.nn = B, xacc, nn

    def xtile(self, t):
        return self.xacc.v[:, t, :], self.xacc.t[t]

    def done(self, t):
        ap, trk = self.xtile(t)
        self.nn.feed(t, ap, trk)

    def finish(self):
        pass


def out_proj_load_w(B, w_out, nk):
    parts = []
    for p0 in range(0, nk, 4):
        wp = B.sb([128, 4, D], BF16, "wo")
        B.dma("pool", wp.v, w_out[p0 * 128:(p0 + 4) * 128, :].rearrange("(j p) n -> p j n", p=128), wp.t,
              writes=[wp.t])
        parts.append(wp)
    return parts


def out_proj_phase(B, C, og, og_t, nk, w_out, xin, sink, wo_parts=None):
    if wo_parts is None:
        wo_parts = out_proj_load_w(B, w_out, nk)
    ogb = [B.sb([128, nk * 128], BF16, "ogb") for _ in range(2)]
    ogT = [B.sb([128, nk, 128], BF16, "ogT") for _ in range(2)]

    def T_stage(t):
        b = t % 2
        xap, xt = sink.xtile(t)
        B.dma("sp", ogb[b].v, og[t * 128:(t + 1) * 128, :], ogb[b].t, reads=og_t[t], writes=[ogb[b].t])
        B.dma("sp", xap, xin.tile(t), xt, reads=[xin.t[t]], writes=[xt])
        for j0 in range(0, nk, 8):
            pb, pb_t = C.next_pb()
            for j in range(j0, min(nk, j0 + 8)):
                B.op("pe", lambda e, j=j, j0=j0, pb=pb, b=b: e.transpose(
                    out=pb[:, (j - j0) * 128:(j - j0 + 1) * 128], in_=ogb[b].v[:, j * 128:(j + 1) * 128],
                    identity=C.ident.v), reads=[ogb[b].t, C.ident.t], writes=[pb_t])
            nj = min(nk, j0 + 8) - j0
            B.op("act", lambda e, j0=j0, nj=nj, pb=pb, b=b: e.copy(
                out=ogT[b].v[:, j0:j0 + nj, :], in_=pb[:, 0:nj * 128].rearrange("p (k n) -> p k n", k=nj)),
                reads=[pb_t], writes=[ogT[b].t])

    def M_stage(t):
        b = t % 2
        xap, xt = sink.xtile(t)
        for half in range(2):
            pf, pf_t = C.next_pf()
            for j in range(nk):
                B.op("pe", lambda e, j=j, half=half, pf=pf, b=b: e.matmul(
                    out=pf[:], lhsT=ogT[b].v[:, j, :], rhs=wo_parts[j // 4].v[:, j % 4, half * 512:(half + 1) * 512],
                    start=(j == 0), stop=(j == nk - 1)), reads=[ogT[b].t, wo_parts[j // 4].t], writes=[pf_t])
            B.op("dve", lambda e, half=half, pf=pf, xap=xap: e.tensor_tensor(
                out=xap[:, half * 512:(half + 1) * 512], in0=xap[:, half * 512:(half + 1) * 512],
                in1=pf[:], op=ALU.add), reads=[pf_t, xt], writes=[xt])
        sink.done(t)

    T_stage(0)
    for t in range(NT):
        if t + 1 < NT:
            T_stage(t + 1)
        M_stage(t)
    B.release(*wo_parts, *ogb, *ogT)
    sink.finish()


def ret_phase(B, C, xin, xout, pos, attn_norm, w_in, out_norm, w_out, K, next_gain=None, next_w_hook=None):
    import math
    nc = B.nc
    og = nc.dram_tensor("og_ret", [S, RH * RDV], BF16, kind="ExternalOutput" if DEBUG else "Internal").ap()
    og_t = [[Trk("og") for _ in range(RH)] for _ in range(NT)]
    cos = B.sb([128, S], F32, "cos")
    sin = B.sb([128, S], F32, "sin")
    posi = B.sb([128, S], I32, "posi")
    posf = B.sb([128, S], F32, "posf")
    inv = B.sb([128, 1], F32, "inv")
    B.dma("sp", posi.v, pos.partition_broadcast(128), posi.t, writes=[posi.t])
    B.dma("sp", inv.v, K["inv128"], inv.t, writes=[inv.t])
    B.op("dve", lambda e: e.tensor_copy(out=posf.v, in_=posi.v), reads=[posi.t], writes=[posf.t])

    def ang(shift):
        def f(a0):
            B.op("dve", lambda e: e.tensor_scalar(out=a0.v, in0=posf.v, scalar1=inv.v[:, 0:1], scalar2=shift,
                                                  op0=ALU.mult, op1=ALU.add), reads=[posf.t, inv.t], writes=[a0.t])
        return f
    sin_table(B, sin.v, sin.t, ang(0.0), [128, S])
    sin_table(B, cos.v, cos.t, ang(math.pi / 2), [128, S])
    B.release(posi, posf, inv)
    qd = B.sb([128, RH, 128], F32, "qd")
    kd = B.sb([128, RH, 128], F32, "kd")
    maskT = B.sb([128, 128], BF16, "maskT")
    gain_o = B.sb([128, RH, RDV], F32, "gain_o")
    B.dma("sp", qd.v, K["qdec"].partition_broadcast(128), qd.t, writes=[qd.t])
    B.dma("sp", kd.v, K["kdec"].partition_broadcast(128), kd.t, writes=[kd.t])
    B.dma("sp", maskT.v, K["maskT"], maskT.t, writes=[maskT.t])
    B.dma("sp", gain_o.v, out_norm.rearrange("h d -> (h d)").partition_broadcast(128), gain_o.t, writes=[gain_o.t])
    hT = B.sb([128, KT, S], BF16, "hT", n=NT)
    load_norm_transpose_all(B, C, xin, attn_norm, hT)
    wq = [B.sb([128, KT, RDK], BF16, "wq") for _ in range(2)]
    wk = [B.sb([128, KT, RDK], BF16, "wk") for _ in range(2)]
    wv = [B.sb([128, KT, RDV], BF16, "wv") for _ in range(2)]
    wg = [B.sb([128, KT, RDV], BF16, "wg") for _ in range(2)]
    qT2 = [B.sb([128, 2, S], BF16, "qT", n=4) for _ in range(2)]
    kT2 = [B.sb([128, 2, S], BF16, "kT", n=4) for _ in range(2)]
    ktm2 = [B.sb([128, NT, RDK], BF16, "ktm", n=4) for _ in range(2)]
    Tst = B.sb([128, 2, RDV], F32, "Tst", n=2)
    Sbf2 = [B.sb([128, 2, RDV], BF16, "Sbf", n=2) for _ in range(2)]
    ab = [[B.sb([128, 512], F32, "ab") for _ in range(2)] for _ in range(2)]
    tm = [B.sb([128, 512], F32, "tm") for _ in range(4)]
    vb = [B.sb([128, RDV], BF16, "vb") for _ in range(3)]
    sg = [B.sb([128, RDV], F32, "sg") for _ in range(3)]
    sgg = [B.sb([128, RDV], F32, "sgg") for _ in range(3)]
    sTm = [B.sb([128, 128], BF16, "sTm") for _ in range(3)]
    ogb = [B.sb([128, RDV], BF16, "ogb") for _ in range(2)]
    ost = [B.sb([128, 4], F32, "ost", n=3) for _ in range(2)]
    ojunk = B.sb([128, RDV], BF16, "ojunk")

    def load_w(h):
        b = h % 2
        for (wt, c0, wd) in ((wq[b], h * RDK, RDK), (wk[b], 1024 + h * RDK, RDK),
                             (wv[b], 2048 + h * RDV, RDV), (wg[b], 4096 + h * RDV, RDV)):
            B.dma("pool", wt.v, w_in[:, c0:c0 + wd].rearrange("(k p) f -> p k f", p=128), wt.t, writes=[wt.t])

    rs_ = {"ab": 0, "q": 0, "c": 0}

    def nbQ():
        i = 4 + rs_["q"]
        rs_["q"] ^= 1
        return C.pf[i], C.pf_t[i]

    def nbC():
        i = rs_["c"]
        rs_["c"] = (i + 1) % 4
        return C.pf[i], C.pf_t[i]

    def stage_QK(h):
        b = h % 2
        qT, kT, ktm = qT2[b], kT2[b], ktm2[b]
        for (wt, dec, dstT) in ((wq[b], qd, qT), (wk[b], kd, kT)):
            for tb in range(4):
                sl = slice(tb * 512, (tb + 1) * 512)
                A, Bm = ab[rs_['ab']]
                rs_['ab'] ^= 1
                for dti, dst in ((0, A), (1, Bm)):
                    pf, pf_t = nbQ()
                    for k in range(KT):
                        B.op("pe", lambda e, k=k, dti=dti, pf=pf, wt=wt, sl=sl: e.matmul(
                            out=pf[:], lhsT=wt.v[:, k, dti * 128:(dti + 1) * 128], rhs=hT.v[:, k, sl],
                            start=(k == 0), stop=(k == KT - 1)),
                            reads=[wt.t] + hT.t[tb * 4:tb * 4 + 4], writes=[pf_t])
                    B.op("dve", lambda e, pf=pf, dst=dst, dec=dec, h=h: e.tensor_tensor(
                        out=dst.v.rearrange("p (a b) -> p a b", a=4), in0=pf[:].rearrange("p (a b) -> p a b", a=4),
                        in1=bc3(dec.v[:, h, :], 4), op=ALU.mult), reads=[pf_t, dec.t], writes=[dst.t])
                B.op("pool", lambda e, A=A, sl=sl: e.tensor_tensor(out=tm[0].v, in0=A.v, in1=cos.v[:, sl], op=ALU.mult),
                     reads=[A.t, cos.t], writes=[tm[0].t])
                B.op("pool", lambda e, Bm=Bm, sl=sl: e.tensor_tensor(out=tm[1].v, in0=Bm.v, in1=sin.v[:, sl], op=ALU.mult),
                     reads=[Bm.t, sin.t], writes=[tm[1].t])
                B.op("dve", lambda e, dstT=dstT, sl=sl: e.tensor_tensor(out=dstT.v[:, 0, sl], in0=tm[0].v, in1=tm[1].v,
                                                                        op=ALU.subtract),
                     reads=[tm[0].t, tm[1].t], writes=[dstT.t[tb]])
                B.op("pool", lambda e, Bm=Bm, sl=sl: e.tensor_tensor(out=tm[2].v, in0=Bm.v, in1=cos.v[:, sl], op=ALU.mult),
                     reads=[Bm.t, cos.t], writes=[tm[2].t])
                B.op("pool", lambda e, A=A, sl=sl: e.tensor_tensor(out=tm[3].v, in0=A.v, in1=sin.v[:, sl], op=ALU.mult),
                     reads=[A.t, sin.t], writes=[tm[3].t])
                B.op("dve", lambda e, dstT=dstT, sl=sl: e.tensor_tensor(out=dstT.v[:, 1, sl], in0=tm[2].v, in1=tm[3].v,
                                                                        op=ALU.add),
                     reads=[tm[2].t, tm[3].t], writes=[dstT.t[tb]])
        for c4 in range(4):
            pb, pb_t = C.next_pb()
            for ci in range(4):
                c = c4 * 4 + ci
                for dti in range(2):
                    B.op("pe", lambda e, c=c, ci=ci, dti=dti, pb=pb: e.transpose(
                        out=pb[:, ci * 256 + dti * 128: ci * 256 + (dti + 1) * 128],
                        in_=kT.v[:, dti, c * 128:(c + 1) * 128], identity=C.ident.v),
                        reads=[kT.t[c4], C.ident.t], writes=[pb_t])
            B.op("act", lambda e, c4=c4, pb=pb: e.copy(
                out=ktm.v[:, c4 * 4:(c4 + 1) * 4, :], in_=pb[:].rearrange("p (c d) -> p c d", c=4)),
                reads=[pb_t], writes=[ktm.t[c4]])

    def stage_CH(h):
        b = h % 2
        qT, kT, ktm = qT2[b], kT2[b], ktm2[b]
        gam = 1.0 - 2.0 ** (-5.0 - h)
        cd = gam ** 128
        for dti in range(2):
            B.op("pool", lambda e, dti=dti: e.memset(Tst.v[:, dti, :], 0.0), writes=[Tst.t[dti]])

        def pre(c):
            cb = c % 3
            csl = slice(c * 128, (c + 1) * 128)
            pf, pf_t = nbC()
            for k in range(KT):
                B.op("pe", lambda e, k=k, pf=pf: e.matmul(
                    out=pf[:], lhsT=hT.v[:, k, csl], rhs=wv[b].v[:, k, :], start=(k == 0), stop=(k == KT - 1)),
                    reads=[wv[b].t, hT.t[c]], writes=[pf_t])
            B.op("act", lambda e, pf=pf: e.copy(out=vb[cb].v, in_=pf[:]), reads=[pf_t], writes=[vb[cb].t])
            pf, pf_t = nbC()
            for k in range(KT):
                B.op("pe", lambda e, k=k, pf=pf: e.matmul(
                    out=pf[:], lhsT=hT.v[:, k, csl], rhs=wg[b].v[:, k, :], start=(k == 0), stop=(k == KT - 1)),
                    reads=[wg[b].t, hT.t[c]], writes=[pf_t])
            B.op("act", lambda e, pf=pf: e.activation(out=sg[cb].v, in_=pf[:], func=AF.Silu),
                 reads=[pf_t], writes=[sg[cb].t])
            B.op("pool", lambda e: e.tensor_tensor(out=sgg[cb].v, in0=sg[cb].v, in1=gain_o.v[:, h, :], op=ALU.mult),
                 reads=[sg[cb].t, gain_o.t], writes=[sgg[cb].t])
            pf, pf_t = nbC()
            for dti in range(2):
                B.op("pe", lambda e, dti=dti, pf=pf: e.matmul(
                    out=pf[:, 0:128], lhsT=kT.v[:, dti, csl], rhs=qT.v[:, dti, csl], start=(dti == 0), stop=(dti == 1)),
                    reads=[kT.t[c // 4], qT.t[c // 4]], writes=[pf_t])
            B.op("dve", lambda e, pf=pf: e.tensor_tensor(out=sTm[cb].v, in0=pf[:, 0:128], in1=maskT.v, op=ALU.mult),
                 reads=[pf_t, maskT.t], writes=[sTm[cb].t])
            sb_ = Sbf2[c % 2]
            if c > 0:
                for dti in range(2):
                    B.op("act", lambda e, dti=dti: e.activation(out=sb_.v[:, dti, :], in_=Tst.v[:, dti, :],
                                                                func=AF.Copy, scale=cd),
                         reads=[Tst.t[dti]], writes=[sb_.t[dti]])
            if c < NT - 1:
                for dti in range(2):
                    pk, pk_t = nbC()
                    B.op("pe", lambda e, dti=dti, pk=pk: e.matmul(
                        out=pk[:], lhsT=ktm.v[:, c, dti * 128:(dti + 1) * 128], rhs=vb[cb].v, start=True, stop=True),
                        reads=[ktm.t[c // 4], vb[cb].t], writes=[pk_t])
                    B.op("dve", lambda e, dti=dti, pk=pk: e.scalar_tensor_tensor(
                        out=Tst.v[:, dti, :], in0=Tst.v[:, dti, :], scalar=cd, in1=pk[:], op0=ALU.mult, op1=ALU.add),
                        reads=[Tst.t[dti], pk_t], writes=[Tst.t[dti]])

        def post(c):
            cb = c % 3
            csl = slice(c * 128, (c + 1) * 128)
            sb_ = Sbf2[c % 2]
            po, po_t = nbC()
            B.op("pe", lambda e: e.matmul(out=po[:], lhsT=sTm[cb].v, rhs=vb[cb].v, start=True, stop=(c == 0)),
                 reads=[sTm[cb].t, vb[cb].t], writes=[po_t])
            if c > 0:
                for dti in range(2):
                    B.op("pe", lambda e, dti=dti: e.matmul(
                        out=po[:], lhsT=qT.v[:, dti, csl], rhs=sb_.v[:, dti, :], start=False, stop=(dti == 1)),
                        reads=[qT.t[c // 4], sb_.t[dti]], writes=[po_t])
            st = ost[c % 2]
            ob = ogb[c % 2]
            B.op("act", lambda e: e.activation(out=ojunk.v, in_=po[:], func=AF.Square,
                                               scale=1.0 / math.sqrt(RDV), accum_out=st.v[:, 0:1]),
                 reads=[po_t], writes=[ojunk.t, st.t[0]])
            B.op("pool", lambda e: e.tensor_scalar(out=st.v[:, 1:2], in0=st.v[:, 0:1], scalar1=EPS,
                                                   scalar2=None, op0=ALU.add), reads=[st.t[0]], writes=[st.t[1]])
            B.op("pool", lambda e: e.tensor_tensor(out=st.v[:, 2:3], in0=st.v[:, 1:2],
                                                   in1=C.mhalf.v[:, 0:1], op=ALU.pow),
                 reads=[st.t[1], C.mhalf.t], writes=[st.t[2]])
            B.op("dve", lambda e: e.scalar_tensor_tensor(
                out=ob.v, in0=po[:], scalar=st.v[:, 2:3], in1=sgg[cb].v, op0=ALU.mult, op1=ALU.mult),
                reads=[po_t, st.t[2], sgg[cb].t], writes=[ob.t])
            B.dma("sp", og[c * 128:(c + 1) * 128, h * RDV:(h + 1) * RDV], ob.v, ob.t,
                  reads=[ob.t], writes=[og_t[c][h]])

        pre(0)
        for c in range(NT):
            if c + 1 < NT:
                pre(c + 1)
            post(c)

    def record(fn, h):
        rec = []
        orig_op, orig_dma = B.op, B.dma
        B.op = lambda *a, **k: rec.append(("op", a, k))
        B.dma = lambda *a, **k: rec.append(("dma", a, k))
        try:
            fn(h)
        finally:
            B.op, B.dma = orig_op, orig_dma
        return [(lambda kind=kind, a=a, k=k: (orig_op if kind == "op" else orig_dma)(*a, **k)) for kind, a, k in rec]

    load_w(0)
    stage_QK(0)
    early_rel = []
    wo_parts = None
    for h in range(RH):
        if h + 1 < RH:
            load_w(h + 1)
        else:
            pb0 = (RH - 2) % 2
            early_rel = [wq[pb0], wk[pb0], wv[pb0], wg[pb0], qT2[pb0], kT2[pb0], ktm2[pb0]]
            B.release(*early_rel)
            wo_parts = out_proj_load_w(B, w_out, RH * RDV // 128)
        X = record(stage_CH, h)
        Y = record(stage_QK, h + 1) if h + 1 < RH else []
        i = j = 0
        while i < len(X) or j < len(Y):
            if j >= len(Y) or (i < len(X) and i * len(Y) <= j * len(X)):
                X[i]()
                i += 1
            else:
                Y[j]()
                j += 1
    B.release(*[x for x in (cos, sin, qd, kd, maskT, gain_o, hT, *wq, *wk, *wv, *wg, *qT2, *kT2, *ktm2, Tst, *Sbf2,
                            *ab[0], *ab[1], *tm, *vb, *sg, *sgg, *sTm, *ogb, *ost, ojunk) if x not in early_rel])
    if next_gain is not None:
        xacc = B.sb([128, NT, D], F32, "xacc", n=NT)
        nn = NextNorm(B, C, next_gain)
        w_pre = next_w_hook() if next_w_hook is not None else None
        out_proj_phase(B, C, og, og_t, RH * RDV // 128, w_out, xin, AccSink(B, xacc, nn), wo_parts=wo_parts)
        return xacc, nn.finish(), w_pre
    out_proj_phase(B, C, og, og_t, RH * RDV // 128, w_out, xin, DramSink(B, xout), wo_parts=wo_parts)
    return None


def bc_last(ap2d, m):
    return ap2d.unsqueeze(2).to_broadcast([ap2d.shape[0], ap2d.shape[1], m])


def dsa_prologue(B, C, pos, w_in, q_norm, k_norm, kidx_norm, K):
    import math
    CB = [(0, 512), (512, 1024), (1024, 1536), (1536, 2048), (2048, DSA_IN)]
    win = B.sb([128, KT, DSA_IN], BF16, "win", n=5)
    for i, (c0, c1) in enumerate(CB):
        B.dma("pool", win.v[:, :, c0:c1], w_in[:, c0:c1].rearrange("(k p) f -> p k f", p=128), win.t[i],
              writes=[win.t[i]])
    posi = C.posi_tm
    posf = B.sb([128, NT], F32, "posf")
    B.op("dve", lambda e: e.tensor_copy(out=posf.v, in_=posi.v), reads=[posi.t], writes=[posf.t])
    tabs = {}
    for nm, half in (("a", 64), ("i", 32)):
        invb = B.sb([128, half], F32, "invb")
        B.dma("sp", invb.v, K["inv%d" % half].partition_broadcast(128), invb.t, writes=[invb.t])
        for fn, shift in (("sin", 0.0), ("cos", math.pi / 2)):
            tab = B.sb([128, NT, half], F32, fn + nm)

            def ang(a0, invb=invb, half=half, shift=shift):
                B.op("dve", lambda e: e.tensor_tensor(out=a0.v, in0=bc_last(posf.v, half), in1=bc3(invb.v, NT),
                                                      op=ALU.mult), reads=[posf.t, invb.t], writes=[a0.t])
                if shift:
                    B.op("dve", lambda e: e.tensor_scalar(out=a0.v, in0=a0.v, scalar1=shift, scalar2=None,
                                                          op0=ALU.add), reads=[a0.t], writes=[a0.t])
            sin_table(B, tab.v, tab.t, ang, [128, NT, half])
            tabs[fn + nm] = tab
        B.release(invb)
    B.release(posf)
    gqk = B.sb([128, AH + AG, ADH], F32, "gqk")
    gq1 = B.sb([128, ADH], F32, "gq1")
    gk1 = B.sb([128, ADH], F32, "gk1")
    gki = B.sb([128, IDH], F32, "gki")
    B.dma("sp", gq1.v, q_norm.partition_broadcast(128), gq1.t, writes=[gq1.t])
    B.dma("sp", gk1.v, k_norm.partition_broadcast(128), gk1.t, writes=[gk1.t])
    B.dma("sp", gki.v, kidx_norm.partition_broadcast(128), gki.t, writes=[gki.t])
    B.op("dve", lambda e: e.tensor_scalar(out=gqk.v[:, 0:AH, :], in0=bc3(gq1.v, AH), scalar1=ADH ** -0.5,
                                          scalar2=None, op0=ALU.mult), reads=[gq1.t], writes=[gqk.t])
    B.op("dve", lambda e: e.tensor_copy(out=gqk.v[:, AH:AH + AG, :], in_=bc3(gk1.v, AG)),
         reads=[gk1.t, gqk.t], writes=[gqk.t])
    B.release(gq1, gk1)
    return {"win": win, "tabs": tabs, "gqk": gqk, "gki": gki, "CB": CB}


def dsa_phase(B, C, xin, xout, pos, attn_norm, w_in, q_norm, k_norm, kidx_norm, w_out, K, hT_pre=None, P=None, next_gain=None):
    import math
    qT = B.sb([128, NT, AH, 128], BF16, "qT", n=NT)
    kT = B.sb([128, AG, S], BF16, "kT", n=NT)
    qiT = B.sb([128, 4, S], BF16, "qiT", n=NT)
    kiT = B.sb([128, S], BF16, "kiT", n=NT)
    vtm = B.sb([128, NT, AG * ADH], BF16, "vtm", n=NT)
    wall = B.sb([128, NT, IH], F32, "wall", n=NT)
    if hT_pre is not None:
        hT = hT_pre
    else:
        hT = B.sb([128, KT, S], BF16, "hT", n=NT)
        load_norm_transpose_all(B, C, xin, attn_norm, hT)
    if P is None:
        P = dsa_prologue(B, C, pos, w_in, q_norm, k_norm, kidx_norm, K)
    win, tabs, gqk, gki, CB = P["win"], P["tabs"], P["gqk"], P["gki"], P["CB"]
    raw = [B.sb([128, DSA_IN], F32, "raw") for _ in range(2)]
    sq = B.sb([128, 1344], F32, "sq")
    st = [B.sb([128, 36], F32, "dst", n=3) for _ in range(2)]
    tmp = [B.sb([128, 640], F32, "rt") for _ in range(4)]
    qkr = [B.sb([128, AH + AG, ADH], BF16, "qkr") for _ in range(2)]
    qkir = [B.sb([128, 10, IDH], BF16, "qkir") for _ in range(2)]
    NQK = (AH + AG) * ADH
    WSC = IH ** -0.5 * IDH ** -0.5
    def st2_P(t):
            tsl = slice(t * 128, (t + 1) * 128)
            r = raw[t % 2]
            s3 = st[t % 2]
            banks = []
            for i, (c0, c1) in enumerate(CB):
                pf, pf_t = C.pf[i], C.pf_t[i]
                for k in range(KT):
                    B.op("pe", lambda e, k=k, pf=pf, c0=c0, c1=c1, tsl=tsl: e.matmul(
                        out=pf[:, 0:c1 - c0], lhsT=hT.v[:, k, tsl], rhs=win.v[:, k, c0:c1],
                        start=(k == 0), stop=(k == KT - 1)), reads=[hT.t[t], win.t[i]], writes=[pf_t])
                B.op("act", lambda e, pf=pf, c0=c0, c1=c1, r=r: e.copy(out=r.v[:, c0:c1], in_=pf[:, 0:c1 - c0]),
                     reads=[pf_t], writes=[r.t])

    def st2_N(t):
            tsl = slice(t * 128, (t + 1) * 128)
            r = raw[t % 2]
            s3 = st[t % 2]
            rqk = r.v[:, 0:NQK].rearrange("p (h d) -> p h d", h=10)
            B.op("act", lambda e, r=r: e.activation(out=sq.v[:, 0:NQK], in_=r.v[:, 0:NQK], func=AF.Square,
                                                    scale=1.0 / math.sqrt(ADH)), reads=[r.t], writes=[sq.t])
            B.op("act", lambda e, r=r: e.activation(out=sq.v[:, NQK:NQK + 64], in_=r.v[:, 2048:2112], func=AF.Square,
                                                    scale=1.0 / math.sqrt(IDH)), reads=[r.t], writes=[sq.t])
            B.op("dve", lambda e, s3=s3: e.tensor_reduce(out=s3.v[:, 0:10], in_=sq.v[:, 0:NQK].rearrange(
                "p (h d) -> p h d", h=10), axis=AX.X, op=ALU.add), reads=[sq.t], writes=[s3.t[0]])
            B.op("dve", lambda e, s3=s3: e.tensor_reduce(out=s3.v[:, 10:11], in_=sq.v[:, NQK:NQK + 64],
                                                         axis=AX.X, op=ALU.add), reads=[sq.t, s3.t[0]], writes=[s3.t[0]])
            B.op("act", lambda e, s3=s3: e.activation(out=s3.v[:, 12:23], in_=s3.v[:, 0:11], func=AF.Sqrt,
                                                      bias=C.eps.v[:, 0:1], scale=1.0),
                 reads=[s3.t[0], C.eps.t], writes=[s3.t[1]])
            B.op("dve", lambda e, s3=s3: e.reciprocal(out=s3.v[:, 24:35], in_=s3.v[:, 12:23]),
                 reads=[s3.t[1]], writes=[s3.t[2]])
            rqk = r.v[:, 0:NQK].rearrange("p (h d) -> p h d", h=10)
            B.op("dve", lambda e, rqk=rqk, s3=s3: e.tensor_tensor(out=rqk, in0=rqk, in1=bc_last(s3.v[:, 24:34], ADH),
                                                                  op=ALU.mult), reads=[r.t, s3.t[2]], writes=[r.t])
            B.op("dve", lambda e, rqk=rqk: e.tensor_tensor(out=rqk, in0=rqk, in1=gqk.v, op=ALU.mult),
                 reads=[r.t, gqk.t], writes=[r.t])
            B.op("dve", lambda e, r=r, s3=s3: e.scalar_tensor_tensor(
                out=r.v[:, 2048:2112], in0=r.v[:, 2048:2112], scalar=s3.v[:, 34:35], in1=gki.v,
                op0=ALU.mult, op1=ALU.mult), reads=[r.t, s3.t[2], gki.t], writes=[r.t])
            B.op("dve", lambda e, r=r, t=t: e.tensor_scalar(out=wall.v[:, t, :], in0=r.v[:, 2112:2120], scalar1=WSC,
                                                            scalar2=None, op0=ALU.mult), reads=[r.t], writes=[wall.t[t]])
            B.op("act", lambda e, r=r, t=t: e.copy(out=vtm.v[:, t, :], in_=r.v[:, 1280:1536]),
                 reads=[r.t], writes=[vtm.t[t]])

    def st2_R(t):
            tsl = slice(t * 128, (t + 1) * 128)
            r = raw[t % 2]
            s3 = st[t % 2]
            rqk = r.v[:, 0:NQK].rearrange("p (h d) -> p h d", h=10)
            for (x3, nh, half, cosT, sinT, dst) in (
                    (rqk, 10, 64, tabs["cosa"], tabs["sina"], qkr[t % 2]),
                    (r.v[:, 1536:2112].rearrange("p (h d) -> p h d", h=9), 9, 32, tabs["cosi"], tabs["sini"],
                     qkir[t % 2])):
                x1, x2 = x3[:, :, 0:half], x3[:, :, half:2 * half]
                cb_, sb_ = bc3(cosT.v[:, t, :], nh), bc3(sinT.v[:, t, :], nh)
                tv = [tmp[i].v[:, 0:nh * half].rearrange("p (h d) -> p h d", h=nh) for i in range(4)]
                B.op("dve", lambda e, x1=x1, cb_=cb_, tv=tv: e.tensor_tensor(out=tv[0], in0=x1, in1=cb_, op=ALU.mult),
                     reads=[r.t, cosT.t], writes=[tmp[0].t])
                B.op("pool", lambda e, x2=x2, sb_=sb_, tv=tv: e.tensor_tensor(out=tv[1], in0=x2, in1=sb_, op=ALU.mult),
                     reads=[r.t, sinT.t], writes=[tmp[1].t])
                B.op("dve", lambda e, x2=x2, cb_=cb_, tv=tv: e.tensor_tensor(out=tv[2], in0=x2, in1=cb_, op=ALU.mult),
                     reads=[r.t, cosT.t], writes=[tmp[2].t])
                B.op("pool", lambda e, x1=x1, sb_=sb_, tv=tv: e.tensor_tensor(out=tv[3], in0=x1, in1=sb_, op=ALU.mult),
                     reads=[r.t, sinT.t], writes=[tmp[3].t])
                B.op("dve", lambda e, dst=dst, nh=nh, half=half, tv=tv: e.tensor_tensor(
                    out=dst.v[:, 0:nh, 0:half], in0=tv[0], in1=tv[1], op=ALU.subtract),
                    reads=[tmp[0].t, tmp[1].t], writes=[dst.t])
                B.op("pool", lambda e, dst=dst, nh=nh, half=half, tv=tv: e.tensor_tensor(
                    out=dst.v[:, 0:nh, half:2 * half], in0=tv[2], in1=tv[3], op=ALU.add),
                    reads=[tmp[2].t, tmp[3].t, dst.t], writes=[dst.t])
            qr, qir = qkr[t % 2], qkir[t % 2]
            B.op("pool", lambda e, qir=qir: e.tensor_copy(out=qir.v[:, 9, :], in_=qir.v[:, 8, :]),
                 reads=[qir.t], writes=[qir.t])
            pb0, pb0_t = C.pb[0], C.pb_t[0]
            pb1, pb1_t = C.pb[1], C.pb_t[1]
            for h in range(AH):
                B.op("pe", lambda e, h=h, qr=qr, pb0=pb0: e.transpose(out=pb0[:, h * 128:(h + 1) * 128], in_=qr.v[:, h, :],
                                                                       identity=C.ident.v),
                     reads=[qr.t, C.ident.t], writes=[pb0_t])
            B.op("act", lambda e, t=t, pb0=pb0: e.copy(out=qT.v[:, t, :, :], in_=pb0[:].rearrange("p (h q) -> p h q", h=AH)),
                 reads=[pb0_t], writes=[qT.t[t]])
            for g in range(AG):
                B.op("pe", lambda e, g=g, qr=qr, pb1=pb1: e.transpose(out=pb1[:, g * 128:(g + 1) * 128],
                                                                       in_=qr.v[:, AH + g, :], identity=C.ident.v),
                     reads=[qr.t, C.ident.t], writes=[pb1_t])
            for p in range(5):
                B.op("pe", lambda e, p=p, qir=qir, pb1=pb1: e.transpose(
                    out=pb1[:, 256 + p * 128:256 + (p + 1) * 128],
                    in_=qir.v[:, 2 * p:2 * p + 2, :].rearrange("p a d -> p (a d)"), identity=C.ident.v),
                    reads=[qir.t, C.ident.t], writes=[pb1_t])
            B.op("act", lambda e, tsl=tsl, pb1=pb1: e.copy(out=kT.v[:, :, tsl], in_=pb1[:, 0:256].rearrange(
                "p (g s) -> p g s", g=AG)), reads=[pb1_t], writes=[kT.t[t]])
            B.op("act", lambda e, tsl=tsl, pb1=pb1: e.copy(out=qiT.v[:, :, tsl], in_=pb1[:, 256:768].rearrange(
                "p (a s) -> p a s", a=4)), reads=[pb1_t], writes=[qiT.t[t]])
            B.op("act", lambda e, tsl=tsl, pb1=pb1: e.copy(out=kiT.v[:, tsl], in_=pb1[:, 768:896]),
                 reads=[pb1_t], writes=[kiT.t[t]])

    class _Rec:
        def __init__(self):
            self.items = []

    def _record(fn, t):
        rec = []
        orig_op, orig_dma = B.op, B.dma
        B.op = lambda *a, **k: rec.append(("op", a, k))
        B.dma = lambda *a, **k: rec.append(("dma", a, k))
        try:
            fn(t)
        finally:
            B.op, B.dma = orig_op, orig_dma
        return [(lambda kind=kind, a=a, k=k: (orig_op if kind == "op" else orig_dma)(*a, **k)) for kind, a, k in rec]

    def _interleave(X, Y):
        out = []
        i = j = 0
        while i < len(X) or j < len(Y):
            if j >= len(Y) or (i < len(X) and i * len(Y) <= j * len(X)):
                out.append(X[i])
                i += 1
            else:
                out.append(Y[j])
                j += 1
        return out

    st2_P(0)
    st2_P(1)
    st2_N(0)
    for t in range(NT):
        X = _record(st2_R, t)
        Y = _record(st2_N, t + 1) if t + 1 < NT else []
        for u in _interleave(X, Y):
            u()
        if t + 2 < NT:
            st2_P(t + 2)
    B.release(hT, win, gqk, gki, sq, *raw, *st, *tmp, *qkr, *qkir, *tabs.values())

    NEG = -30000.0
    wo = B.sb([128, AH, D], BF16, "wo")
    B.dma("pool", wo.v, w_out.rearrange("(j p) n -> p j n", p=128), wo.t, writes=[wo.t])
    negm = B.sb([128, 128], F32, "negm")
    pw2 = B.sb([128, NBIS + 1], F32, "pw2")
    ones = B.sb([128, 128], BF16, "ones")
    B.dma("sp", negm.v, K["negm"], negm.t, writes=[negm.t])
    B.dma("sp", pw2.v, K["pw2"].partition_broadcast(128), pw2.t, writes=[pw2.t])
    B.op("pool", lambda e: e.memset(ones.v, 1.0), writes=[ones.t])
    tau0 = B.sb([128, 1], F32, "tau0")
    B.op("pool", lambda e: e.memset(tau0.v, -1e29), writes=[tau0.t])
    score = [B.sb([128, S], F32, "score", n=4) for _ in range(4)]
    rl = [B.sb([128, 512], F32, "rl") for _ in range(3)]
    junk_d = B.sb([128, S], BF16, "junkd")
    junk_a = B.sb([128, S], BF16, "junka")
    nmask = [B.sb([128, S], BF16, "nmask") for _ in range(2)]
    nmT = [B.sb([128, NT, 128], BF16, "nmT", n=2) for _ in range(4)]
    WO = 8
    NWO = WO + NBIS + 1
    CBC = NWO + NBIS + 1
    bst = [B.sb([128, CBC + 1], F32, "bst", n=10) for _ in range(2)]
    ebuf = [B.sb([128, 512], BF16, "ebuf") for _ in range(4)]
    rden = B.sb([128, 512], F32, "rden")
    lnd = B.sb([128, 512], F32, "lnd")
    aoT = [B.sb([128, AH, 128], BF16, "aoT", n=2) for _ in range(2)]
    xb = [B.sb([128, D], F32, "xb") for _ in range(2)]
    rot = {"i": 0, "rl": 0, "e": 0}
    nnx = {"nn": None}

    def next_bank():
        i = rot["i"]
        rot["i"] = (i + 1) % 4
        return C.pf[i], C.pf_t[i]

    def stage_AB(qt, part):
        units = []
        qsl = slice(qt * 128, (qt + 1) * 128)
        nk = qt + 1
        nkeys = nk * 128
        sc_, nm, nT, bs = score[qt % 4], nmask[qt % 2], nmT[qt % 4], bst[qt % 2]
        nkb = (nkeys + 511) // 512
        sct = sc_.t[0:nkb]
        use_act = (qt % 2 == 1)

        def idx_unit(kb, hi):
            def f():
                k0, k1 = kb * 512, min(nkeys, (kb + 1) * 512)
                po = 64 * (hi % 2)
                pf, pf_t = next_bank()
                B.op("pe", lambda e: e.matmul(
                    out=pf[:, 0:k1 - k0], lhsT=qiT.v[po:po + 64, hi // 2, qsl], rhs=kiT.v[po:po + 64, k0:k1],
                    start=True, stop=True),
                    reads=[qiT.t[qt]] + kiT.t[k0 // 128:(k1 + 127) // 128], writes=[pf_t])
                if hi == 0:
                    B.op("dve", lambda e: e.tensor_scalar(
                        out=sc_.v[:, k0:k1], in0=pf[:, 0:k1 - k0], scalar1=0.0, scalar2=wall.v[:, qt, 0:1],
                        op0=ALU.max, op1=ALU.mult), reads=[pf_t, wall.t[qt]], writes=[sc_.t[kb]])
                else:
                    rr = rl[rot["rl"]]
                    rot["rl"] = (rot["rl"] + 1) % 3
                    B.op("dve", lambda e: e.tensor_scalar(
                        out=rr.v[:, 0:k1 - k0], in0=pf[:, 0:k1 - k0], scalar1=0.0, scalar2=wall.v[:, qt, hi:hi + 1],
                        op0=ALU.max, op1=ALU.mult), reads=[pf_t, wall.t[qt]], writes=[rr.t])
                    B.op("pool", lambda e: e.tensor_tensor(
                        out=sc_.v[:, k0:k1], in0=sc_.v[:, k0:k1], in1=rr.v[:, 0:k1 - k0], op=ALU.add),
                        reads=[rr.t, sc_.t[kb]], writes=[sc_.t[kb]])
            f.w = 1.0
            return f
        if part == "A":
            for kb in range(nkb):
                for hi in range(IH):
                    units.append(idx_unit(kb, hi))
            return units

        dkb = (qt * 128) // 512

        def prelude():
            if qt >= 2:
                B.op("dve", lambda e: e.tensor_reduce(out=bs.v[:, 0:1], in_=sc_.v[:, 0:qt * 128], axis=AX.X,
                                                      op=ALU.min), reads=sct, writes=[bs.t[0]])
            B.op("dve", lambda e: e.tensor_tensor(out=sc_.v[:, qsl], in0=sc_.v[:, qsl], in1=negm.v, op=ALU.add),
                 reads=[sc_.t[dkb], negm.t], writes=[sc_.t[dkb]])
            if qt < 2:
                return
            B.op("dve", lambda e: e.tensor_reduce(out=bs.v[:, 1:2], in_=sc_.v[:, 0:nkeys], axis=AX.X, op=ALU.max),
                 reads=sct, writes=[bs.t[1]])
            B.op("dve", lambda e: e.tensor_tensor(out=bs.v[:, 2:3], in0=bs.v[:, 1:2], in1=bs.v[:, 0:1],
                                                  op=ALU.subtract), reads=[bs.t[0], bs.t[1]], writes=[bs.t[2]])
            B.op("dve", lambda e: e.tensor_scalar(out=bs.v[:, WO:WO + NBIS + 1], in0=pw2.v, scalar1=bs.v[:, 2:3],
                                                  scalar2=None, op0=ALU.mult),
                 reads=[bs.t[2], pw2.t], writes=[bs.t[3]])
            if use_act:
                B.op("dve", lambda e: e.tensor_scalar(out=bs.v[:, NWO:NWO + NBIS + 1], in0=bs.v[:, WO:WO + NBIS + 1],
                                                      scalar1=-1.0, scalar2=None, op0=ALU.mult),
                     reads=[bs.t[3]], writes=[bs.t[3]])
                B.op("dve", lambda e: e.tensor_scalar(out=bs.v[:, 3:4], in0=bs.v[:, 0:1], scalar1=-1.0,
                                                      scalar2=bs.v[:, WO:WO + 1], op0=ALU.mult, op1=ALU.subtract),
                     reads=[bs.t[0], bs.t[3]], writes=[bs.t[4]])
                B.op("pool", lambda e: e.memset(bs.v[:, CBC:CBC + 1], float(nkeys) - 2.0 * TOPK + 0.5),
                     writes=[bs.t[9]])
            else:
                B.op("dve", lambda e: e.tensor_tensor(out=bs.v[:, 3:4], in0=bs.v[:, 0:1], in1=bs.v[:, WO:WO + 1],
                                                      op=ALU.add), reads=[bs.t[0], bs.t[3]], writes=[bs.t[4]])
        prelude.w = 6.0
        units.append(prelude)

        def it_dve(k):
            def f():
                B.op("dve", lambda e: e.tensor_scalar(
                    out=junk_d.v[:, 0:nkeys], in0=sc_.v[:, 0:nkeys], scalar1=bs.v[:, 3:4], scalar2=None,
                    op0=ALU.is_ge, op1=ALU.add, accum_out=bs.v[:, 4:5]),
                    reads=sct + [bs.t[4]], writes=[junk_d.t, bs.t[5]])
                B.op("dve", lambda e: e.tensor_scalar(
                    out=bs.v[:, 5:6], in0=bs.v[:, 4:5], scalar1=float(TOPK), scalar2=bs.v[:, WO + k:WO + k + 1],
                    op0=ALU.is_ge, op1=ALU.mult), reads=[bs.t[5], bs.t[3]], writes=[bs.t[6]])
                B.op("dve", lambda e: e.scalar_tensor_tensor(
                    out=bs.v[:, 3:4], in0=bs.v[:, 5:6], scalar=bs.v[:, WO + k + 1:WO + k + 2], in1=bs.v[:, 3:4],
                    op0=ALU.subtract, op1=ALU.add), reads=[bs.t[6], bs.t[3], bs.t[4]], writes=[bs.t[4]])
            f.w = 2.5
            return f

        def it_act(k):
            cur, cur_t = (3, 4) if k % 2 == 0 else (7, 8)
            nxt, nxt_t = (7, 8) if k % 2 == 0 else (3, 4)

            def f():
                B.op("act", lambda e: e.activation(
                    out=junk_a.v[:, 0:nkeys], in_=sc_.v[:, 0:nkeys], func=AF.Sign, bias=bs.v[:, cur:cur + 1],
                    scale=1.0, accum_out=bs.v[:, 4:5]),
                    reads=sct + [bs.t[cur_t]], writes=[junk_a.t, bs.t[5]])
                B.op("act", lambda e: e.activation(out=bs.v[:, 5:6], in_=bs.v[:, 4:5], func=AF.Sign,
                                                   bias=bs.v[:, CBC:CBC + 1], scale=1.0),
                     reads=[bs.t[5], bs.t[9]], writes=[bs.t[6]])
                B.op("act", lambda e: e.activation(out=bs.v[:, nxt:nxt + 1], in_=bs.v[:, 5:6], func=AF.Identity,
                                                   bias=bs.v[:, cur:cur + 1],
                                                   scale=bs.v[:, NWO + k + 1:NWO + k + 2]),
                     reads=[bs.t[6], bs.t[3], bs.t[cur_t]], writes=[bs.t[nxt_t]])
            f.w = 2.5
            return f
        if qt >= 2:
            for k in range(NBIS):
                units.append(it_act(k) if use_act else it_dve(k))

        def finish():
            if qt >= 2:
                if use_act:
                    fin, fin_t = (3, 4) if NBIS % 2 == 0 else (7, 8)
                    B.op("dve", lambda e: e.tensor_scalar(out=bs.v[:, 6:7], in0=bs.v[:, fin:fin + 1], scalar1=-1.0,
                                                          scalar2=bs.v[:, WO + NBIS:WO + NBIS + 1],
                                                          op0=ALU.mult, op1=ALU.subtract),
                         reads=[bs.t[fin_t], bs.t[3]], writes=[bs.t[7]])
                else:
                    B.op("dve", lambda e: e.tensor_tensor(out=bs.v[:, 6:7], in0=bs.v[:, 3:4],
                                                          in1=bs.v[:, WO + NBIS:WO + NBIS + 1], op=ALU.subtract),
                         reads=[bs.t[4], bs.t[3]], writes=[bs.t[7]])
                tau_ap, tau_t = bs.v[:, 6:7], bs.t[7]
            else:
                tau_ap, tau_t = tau0.v[:, 0:1], tau0.t
            B.op("dve", lambda e: e.tensor_scalar(
                out=nm.v[:, 0:nkeys], in0=sc_.v[:, 0:nkeys], scalar1=tau_ap, scalar2=NEG, op0=ALU.is_lt,
                op1=ALU.mult), reads=sct + [tau_t], writes=[nm.t])
            for k8 in range(0, nk, 8):
                n8 = min(nk, k8 + 8) - k8
                pb, pb_t = C.next_pb()
                for j in range(n8):
                    B.op("pe", lambda e, j=j, k8=k8, pb=pb: e.transpose(
                        out=pb[:, j * 128:(j + 1) * 128], in_=nm.v[:, (k8 + j) * 128:(k8 + j + 1) * 128],
                        identity=C.ident.v), reads=[nm.t, C.ident.t], writes=[pb_t])
                B.op("act", lambda e, k8=k8, n8=n8, pb=pb: e.copy(
                    out=nT.v[:, k8:k8 + n8, :], in_=pb[:, 0:n8 * 128].rearrange("p (k q) -> p k q", k=n8)),
                    reads=[pb_t], writes=[nT.t[k8 // 8]])
        finish.w = 4.0
        units.append(finish)
        return units

    def stage_C(qt):
        units = []
        nk = qt + 1
        nT = nmT[qt % 4]
        ao = aoT[qt % 2]

        def L_unit(g, kt, eb):
            def f():
                ksl = slice(kt * 128, (kt + 1) * 128)
                pf, pf_t = next_bank()
                B.op("pe", lambda e: e.matmul(
                    out=pf[:], lhsT=kT.v[:, g, ksl],
                    rhs=qT.v[:, qt, 4 * g:4 * g + 4, :].rearrange("p h q -> p (h q)"), start=True, stop=False),
                    reads=[kT.t[kt], qT.t[qt]], writes=[pf_t])
                B.op("pe", lambda e: e.matmul(
                    out=pf[:].rearrange("p (h q) -> p h q", h=4), lhsT=C.ident.v, rhs=bc3(nT.v[:, kt, :], 4),
                    start=False, stop=True), reads=[C.ident.t, nT.t[kt // 8]], writes=[pf_t])
                B.op("act", lambda e: e.activation(out=eb.v, in_=pf[:], func=AF.Exp), reads=[pf_t], writes=[eb.t])
            return f

        def P_unit(g, kt, eb):
            def f():
                B.op("pe", lambda e: e.matmul(
                    out=C.pf[4][:], lhsT=vtm.v[:, kt, g * ADH:(g + 1) * ADH], rhs=eb.v,
                    start=(kt == 0), stop=(kt == nk - 1)), reads=[vtm.t[kt], eb.t], writes=[C.pf_t[4]])
                B.op("pe", lambda e: e.matmul(
                    out=C.pf[5][:], lhsT=ones.v, rhs=eb.v, start=(kt == 0), stop=(kt == nk - 1)),
                    reads=[ones.t, eb.t], writes=[C.pf_t[5]])
            f.w = 0.5
            return f

        def evac_unit(g):
            def f():
                B.op("act", lambda e: e.activation(out=lnd.v, in_=C.pf[5][:], func=AF.Ln), reads=[C.pf_t[5]],
                     writes=[lnd.t])
                B.op("act", lambda e: e.activation(out=rden.v, in_=lnd.v, func=AF.Exp, scale=-1.0), reads=[lnd.t],
                     writes=[rden.t])
                B.op("dve", lambda e: e.tensor_tensor(
                    out=ao.v[:, 4 * g:4 * g + 4, :].rearrange("p h q -> p (h q)"), in0=C.pf[4][:], in1=rden.v,
                    op=ALU.mult), reads=[C.pf_t[4], rden.t], writes=[ao.t[g]])
            f.w = 2.0
            return f
        LAG = 2
        seq = [(g, kt) for g in range(AG) for kt in range(nk)]
        ebs = []
        for i in range(len(seq) + LAG):
            if i < len(seq):
                eb = ebuf[rot["e"]]
                rot["e"] = (rot["e"] + 1) % len(ebuf)
                ebs.append(eb)
                units.append(L_unit(seq[i][0], seq[i][1], eb))
            j = i - LAG
            if j >= 0:
                g, kt = seq[j]
                units.append(P_unit(g, kt, ebs[j]))
                if kt == nk - 1:
                    units.append(evac_unit(g))

        def outproj():
            xbt = xb[qt % 2]
            B.dma("sp", xbt.v, xin.tile(qt), xbt.t, reads=[xin.t[qt]], writes=[xbt.t])
            for half in range(2):
                pf, pf_t = next_bank()
                for h in range(AH):
                    B.op("pe", lambda e, h=h, half=half, pf=pf: e.matmul(
                        out=pf[:], lhsT=ao.v[:, h, :], rhs=wo.v[:, h, half * 512:(half + 1) * 512],
                        start=(h == 0), stop=(h == AH - 1)), reads=[ao.t[h // 4], wo.t], writes=[pf_t])
                B.op("dve", lambda e, half=half, pf=pf: e.tensor_tensor(
                    out=xbt.v[:, half * 512:(half + 1) * 512], in0=xbt.v[:, half * 512:(half + 1) * 512], in1=pf[:],
                    op=ALU.add), reads=[pf_t, xbt.t], writes=[xbt.t])
            B.dma("sp", xout.tile(qt), xbt.v, xbt.t, reads=[xbt.t], writes=[xout.t[qt]])
            if nnx["nn"] is not None:
                nnx["nn"].feed(qt, xbt.v, xbt.t)
        outproj.w = 4.0
        units.append(outproj)
        return units

    def merge(*streams):
        pairs = [(s if isinstance(s, tuple) else (s, 1.0)) for s in streams]
        pairs = [(s, sp) for s, sp in pairs if s]
        streams = [s for s, sp in pairs]
        tot = [sum(getattr(u, "w", 1.0) for u in s) * 1.0 / sp for s, sp in pairs]
        done = [0.0] * len(streams)
        idx = [0] * len(streams)
        out = []
        while True:
            best, bf = None, None
            for k, s in enumerate(streams):
                if idx[k] < len(s):
                    fr = done[k] / tot[k]
                    if bf is None or fr < bf:
                        best, bf = k, fr
            if best is None:
                break
            u = streams[best][idx[best]]
            out.append(u)
            done[best] += getattr(u, "w", 1.0)
            idx[best] += 1
        return out

    def zip_units(a, b):
        out = []
        for i in range(max(len(a), len(b))):
            if i < len(a):
                out.append(a[i])
            if i < len(b):
                out.append(b[i])
        return out

    for u in stage_AB(0, "A") + stage_AB(1, "A"):
        u()
    NP = NT // 2
    early = []
    for m in range(NP + 1):
        X = (stage_AB(2 * m + 2, "A") + stage_AB(2 * m + 3, "A")) if m + 1 < NP else []
        Bm = zip_units(stage_AB(2 * m, "B"), stage_AB(2 * m + 1, "B")) if m < NP else []
        Cm = (stage_C(2 * m - 2) + stage_C(2 * m - 1)) if m >= 1 else []
        Z = []
        if m == NP and next_gain is not None:
            early = [qiT, kiT, *score, *rl, junk_d, junk_a, *nmask, *bst]
            B.release(*early)
            nnx["nn"] = NextNorm(B, C, next_gain)
            xz = [B.sb([128, D], F32, "xz") for _ in range(2)]

            def z_unit(t):
                def f():
                    xt_ = xz[t % 2]
                    B.dma("sp", xt_.v, xout.tile(t), xt_.t, reads=[xout.t[t]], writes=[xt_.t])
                    nnx["nn"].feed(t, xt_.v, xt_.t)
                return f
            Z = [z_unit(t) for t in range(NT - 2)]
        for u in merge(X, (Bm, 1.25), Cm, Z):
            u()
    hT_next = None
    if nnx["nn"] is not None:
        hT_next = nnx["nn"].finish()
        B.release(*xz)
    B.release(*[x for x in (qT, kT, qiT, kiT, vtm, wall, wo, negm, pw2, ones, tau0, *score, *rl, junk_d, junk_a,
                            *nmask, *nmT, *bst, *ebuf, rden, lnd, *aoT, *xb) if x not in early])
    return hT_next


ALL_INPUTS = [("x", [S, D], F32), ("positions", [S], I32), ("attn_norm", [2, D], F32),
              ("ret_w_in", [1, D, RET_IN], F32), ("ret_out_norm", [1, RH, RDV], F32),
              ("ret_w_out", [1, RH * RDV, D], F32), ("dsa_w_in", [1, D, DSA_IN], F32),
              ("dsa_q_norm", [1, ADH], F32), ("dsa_k_norm", [1, ADH], F32), ("dsa_kidx_norm", [1, IDH], F32),
              ("dsa_w_out", [1, AH * ADH, D], F32), ("mlp_norm", [2, D], F32),
              ("mlp_w_up", [2, D, DFF], F32), ("mlp_w_down", [2, DFF, D], F32)]


def build(phases=("ret", "mlp0", "dsa", "mlp1")):
    nc = bass.Bass("TRN2", target_bir_lowering=False)
    es = ExitStack()
    with es:
        B = Builder(nc, es)
        dt = nc.dram_tensor
        I = {name: dt(name, shape, dty, kind="ExternalInput").ap() for name, shape, dty in ALL_INPUTS}
        K = {name: dt(name, list(arr.shape), BF16 if arr.dtype != np.float32 else F32, kind="ExternalInput").ap()
             for name, arr in _consts().items()}
        out = dt("out", [S, D], F32, kind="ExternalOutput").ap()
        C = Common(B, K)
        C.posi_tm = B.sb([128, NT], I32, "posi_tm")

        def load_posi_tm():
            for t in range(NT):
                B.sc.dma("sp", lambda e, t=t: e.dma_start(
                    out=C.posi_tm.v[:, t:t + 1],
                    in_=I["positions"][t * 128:(t + 1) * 128].rearrange("(p o) -> p o", o=1)),
                    C.posi_tm.t, writes=[C.posi_tm.t])
        if phases[0] == "dsa":
            load_posi_tm()
        cur = DX(I["x"])
        pre = None
        w_pre_next = None
        for i, ph in enumerate(phases):
            last = i == len(phases) - 1
            nxt = phases[i + 1] if not last else None
            dst = DX(out if last else dt("xs%d" % i, [S, D], F32, kind="Internal").ap())
            if ph in ("mlp0", "mlp1"):
                l = int(ph[-1])
                hook = None
                if nxt == "dsa":
                    hook = lambda: dsa_prologue(B, C, I["positions"], I["dsa_w_in"][0], I["dsa_q_norm"][0],
                                                I["dsa_k_norm"][0], I["dsa_kidx_norm"][0], K)
                pre = mlp_phase(B, C, cur, dst, I["mlp_norm"][l], I["mlp_w_up"][l], I["mlp_w_down"][l], pre=pre,
                                next_gain=I["attn_norm"][1] if nxt == "dsa" else None, tail_hook=hook,
                                w_pre=w_pre_next)
                w_pre_next = None
                if nxt != "dsa":
                    pre = None
            elif ph == "ret":
                ng = I["mlp_norm"][int(nxt[-1])] if nxt in ("mlp0", "mlp1") else None
                hk = None
                if ng is not None:
                    lr = int(nxt[-1])
                    hk = lambda: mlp_prefetch_w0(B, I["mlp_w_up"][lr], I["mlp_w_down"][lr])
                pre = ret_phase(B, C, cur, dst, I["positions"], I["attn_norm"][0], I["ret_w_in"][0],
                                I["ret_out_norm"][0], I["ret_w_out"][0], K, next_gain=ng, next_w_hook=hk)
                if pre is not None:
                    w_pre_next = pre[2]
                    pre = (pre[0], pre[1])
            elif ph == "dsa":
                ng = I["mlp_norm"][int(nxt[-1])] if nxt in ("mlp0", "mlp1") else None
                hTn = dsa_phase(B, C, cur, dst, I["positions"], I["attn_norm"][1], I["dsa_w_in"][0],
                                I["dsa_q_norm"][0], I["dsa_k_norm"][0], I["dsa_kidx_norm"][0], I["dsa_w_out"][0], K,
                                hT_pre=pre[0] if pre else None, P=pre[1] if pre else None, next_gain=ng)
                pre = (None, hTn) if hTn is not None else None
            else:
                raise ValueError(ph)
            cur = dst
            if i == 0 and phases[0] != "dsa":
                load_posi_tm()
        B.sc.wait_all("sp", cur.t)
        B.sc.emit()
        print("SBUF peak bytes/partition:", B.peak, "ops:", {k: len(v) for k, v in B.sc.ops.items()},
              "dsems:", B.sc.ndsem)
    return nc


_CONSTS = None


def _consts():
    global _CONSTS
    if _CONSTS is None:
        import ml_dtypes
        bf = ml_dtypes.bfloat16
        c = {}
        c["ident"] = np.eye(128, dtype=np.float32).astype(bf)
        c["inv128"] = (10000.0 ** (-np.arange(128, dtype=np.float64) / 128)).astype(np.float32).reshape(128, 1)
        i = np.arange(128, dtype=np.float64)
        gam = 1.0 - 2.0 ** (-5.0 - np.arange(RH, dtype=np.float64))
        c["qdec"] = (gam[:, None] ** (i[None, :] + 1.0)).astype(np.float32).reshape(-1)
        c["kdec"] = (gam[:, None] ** (-(i[None, :] + 1.0)) * RDK ** -0.5).astype(np.float32).reshape(-1)
        c["maskT"] = (i[None, :] >= i[:, None]).astype(np.float32).astype(bf)
        c["inv64"] = (10000.0 ** (-np.arange(64, dtype=np.float64) / 64)).astype(np.float32)
        c["inv32"] = (10000.0 ** (-np.arange(32, dtype=np.float64) / 32)).astype(np.float32)
        c["negm"] = np.where(i[None, :] > i[:, None], -1e30, 0.0).astype(np.float32)
        c["pw2"] = (2.0 ** -(np.arange(NBIS + 1, dtype=np.float64) + 1.0)).astype(np.float32)
        _CONSTS = c
    return _CONSTS


def run(phases, inputs, x_override=None):
    nc = build(phases)
    c = _consts()
    in_maps = []
    xs = inputs["x"] if x_override is None else x_override
    for b in range(NCORES):
        m = {}
        for name, shape, dty in ALL_INPUTS:
            if name == "x":
                m[name] = np.ascontiguousarray(xs[b])
            elif name == "positions":
                m[name] = np.ascontiguousarray(inputs[name][b]).astype(np.int32)
            else:
                m[name] = np.ascontiguousarray(inputs[name])
        m.update(c)
        in_maps.append(m)
    res = run_bass_kernel_spmd(nc, in_maps, core_ids=list(range(NCORES)))
    if DEBUG:
        LAST["res"] = res.results
    return np.stack([np.asarray(r["out"]) for r in res.results], axis=0)


def kernel(**inputs):
    inputs = {k: np.asarray(v) for k, v in inputs.items()}
    return run(("ret", "mlp0", "dsa", "mlp1"), inputs)
```
